# Optimizing a Trainium2 kernel written in Bass

```python
import math
import jax, jax.numpy as jnp
from jax import lax
import numpy as np

D_MODEL = 1024
BATCH = 32
SEQ = 256
DEPTH = 2
DEC_BATCH = 4
DEC_SEQ = 4096
PAST_LEN = 512

GRID_W = 64
D_MIX = D_MODEL
LRU_W = D_MIX // 2
LRU_BLOCKS = 8
LRU_BW = LRU_W // LRU_BLOCKS
LRU_CONV = 4
LRU_C = 8.0
NA_HEADS = 4
NA_HD = 64
NA_W = NA_HEADS * NA_HD
NA_ROWS = 8
NA_COLS = 16
DF_HEADS = 4
DF_HD = 32
DF_VD = 2 * DF_HD
DF_QK_W = DF_HEADS * 2 * DF_HD
DF_V_W = DF_HEADS * DF_VD
D_IN = 2 * LRU_W + 3 * NA_W + 2 * DF_QK_W + DF_V_W
D_CAT = LRU_W + NA_W + DF_V_W
D_FF = 2816
FFN_CONV = 3
QBLOCK = 128
ROPE_BASE = 10000.0
EPS = 1e-6
NEG_INF = -1e30
N_MOD = 6

kernel_name = "hybrid_diffusion_prefix_trunk_step"


def rms_norm(x, g):
    xf = x.astype(jnp.float32)
    y = xf * lax.rsqrt(jnp.mean(xf * xf, axis=-1, keepdims=True) + EPS)
    return (y * g.astype(jnp.float32)).astype(x.dtype)


def modulate(x, shift, scale):
    return x * (1 + scale) + shift


def dwconv_centred(x, w, b):
    k = w.shape[0]
    left = (k - 1) // 2
    y = lax.conv_general_dilated(x, w[:, None, :].astype(x.dtype), window_strides=(1,),
                                 padding=[(left, k - 1 - left)],
                                 dimension_numbers=('NWC', 'WIO', 'NWC'),
                                 feature_group_count=x.shape[-1])
    return y + b.astype(x.dtype)


def in_projection(h, w_in):
    z = h @ w_in
    sizes = (LRU_W, LRU_W, NA_W, NA_W, NA_W, DF_QK_W, DF_QK_W, DF_V_W)
    offs = [int(o) for o in np.cumsum(sizes)[:-1]]
    return jnp.split(z, offs, axis=-1)


def merge_heads(y):
    b, hh, t, d = y.shape
    return y.transpose(0, 2, 1, 3).reshape(b, t, hh * d)


def rglru_coeffs(x, wa, ba, wx, bx, lam):
    b, t, w = x.shape
    xf = x.astype(jnp.float32)
    xb = xf.reshape(b, t, LRU_BLOCKS, LRU_BW)
    r = jax.nn.sigmoid(jnp.einsum('btni,nij->btnj', xb, wa.astype(jnp.float32)).reshape(b, t, w) + ba.astype(jnp.float32))
    i = jax.nn.sigmoid(jnp.einsum('btni,nij->btnj', xb, wx.astype(jnp.float32)).reshape(b, t, w) + bx.astype(jnp.float32))
    log_a = -LRU_C * r * jax.nn.softplus(-lam.astype(jnp.float32))
    a = jnp.exp(log_a)
    return a, jnp.sqrt(-jnp.expm1(2.0 * log_a)) * (i * xf)


def linear_scan(a, b, h0, reverse):
    end = -1 if reverse else 0
    b = b.at[:, end].add(a[:, end] * h0.astype(jnp.float32))

    def combine(e1, e2):
        a1, b1 = e1
        a2, b2 = e2
        return a1 * a2, a2 * b1 + b2

    _, h = lax.associative_scan(combine, (a, b), reverse=reverse, axis=1)
    return h


def lru_bidir(x, gate, p, h0f, h0b):
    af, bf = rglru_coeffs(x, p['lru_wa'][0], p['lru_ba'][0], p['lru_wx'][0], p['lru_bx'][0], p['lru_lambda'][0])
    hf = linear_scan(af, bf, h0f, False)
    ab, bb = rglru_coeffs(x, p['lru_wa'][1], p['lru_ba'][1], p['lru_wx'][1], p['lru_bx'][1], p['lru_lambda'][1])
    hb = linear_scan(ab, bb, h0b, True)
    y = ((hf + hb) * jax.nn.gelu(gate.astype(jnp.float32))).astype(x.dtype)
    return y, hf[:, -1], hb[:, 0]


def blocked_attention(q, k, v):
    nq, d = q.shape[-2], q.shape[-1]
    nb = nq // QBLOCK
    qb = jnp.moveaxis(q.reshape(q.shape[:-2] + (nb, QBLOCK, d)), -3, 0)
    kt = jnp.swapaxes(k, -1, -2)

    def one_block(qi):
        s = jnp.matmul(qi, kt).astype(jnp.float32)
        return jnp.matmul(jax.nn.softmax(s, axis=-1).astype(v.dtype), v)

    o = jnp.moveaxis(lax.map(one_block, qb), 0, -3)
    return o.reshape(o.shape[:-3] + (nq, o.shape[-1]))


def rope_axis(x, pos):
    half = x.shape[-1] // 2
    freqs = ROPE_BASE ** (-jnp.arange(half, dtype=jnp.float32) / half)
    ang = pos[:, None].astype(jnp.float32) * freqs[None, :]
    cos, sin = jnp.cos(ang).astype(x.dtype), jnp.sin(ang).astype(x.dtype)
    x1, x2 = x[..., :half], x[..., half:]
    return jnp.concatenate([x1 * cos - x2 * sin, x2 * cos + x1 * sin], axis=-1)


def rope_2d(x):
    t = jnp.arange(x.shape[-2])
    n = x.shape[-1] // 2
    return jnp.concatenate([rope_axis(x[..., :n], t // GRID_W), rope_axis(x[..., n:], t % GRID_W)], axis=-1)


def na_qkv(q, k, v, p):
    b, t, _ = q.shape
    q = rms_norm(q.reshape(b, t, NA_HEADS, NA_HD), p['na_q_norm']) * (NA_HD ** -0.5)
    k = rms_norm(k.reshape(b, t, NA_HEADS, NA_HD), p['na_k_norm'])
    v = v.reshape(b, t, NA_HEADS, NA_HD)
    return q.transpose(0, 2, 1, 3), k.transpose(0, 2, 1, 3), v.transpose(0, 2, 1, 3)


def df_qkv(q, k, v, p):
    b, t, _ = q.shape
    q = rms_norm(q.reshape(b, t, DF_HEADS, 2, DF_HD), p['df_q_norm']).transpose(0, 2, 3, 1, 4)
    k = rms_norm(k.reshape(b, t, DF_HEADS, 2, DF_HD), p['df_k_norm']).transpose(0, 2, 3, 1, 4)
    v = v.reshape(b, t, DF_HEADS, DF_VD).transpose(0, 2, 1, 3)
    return q, k, v


def diff_combine(o, p, lam_init):
    lv = p['df_lambda'].astype(jnp.float32)
    lam = jnp.exp(jnp.sum(lv[0] * lv[1])) - jnp.exp(jnp.sum(lv[2] * lv[3])) + lam_init
    y = o[:, :, 0] - lam.astype(o.dtype) * o[:, :, 1]
    return rms_norm(y, p['df_subln']) * (1.0 - lam_init)


def neighbourhood_attention(q, k, v, ck, cv, rpb):
    b, hh, t, d = q.shape
    rows_n = t // GRID_W
    kr = min(NA_ROWS, rows_n)
    ncb = GRID_W // NA_COLS
    kbw = 2 * NA_COLS
    rows = jnp.arange(rows_n)
    row_idx = jnp.clip(rows - kr // 2, 0, rows_n - kr)[:, None] + jnp.arange(kr)[None, :]
    cb = jnp.arange(ncb)
    col_idx = jnp.clip(cb * NA_COLS - NA_COLS // 2, 0, GRID_W - kbw)[:, None] + jnp.arange(kbw)[None, :]
    qcol = cb[:, None] * NA_COLS + jnp.arange(NA_COLS)[None, :]
    cstart = jnp.clip(qcol - NA_COLS // 2, 0, GRID_W - NA_COLS)
    kc = col_idx[:, None, :]
    in_win = (kc >= cstart[:, :, None]) & (kc < cstart[:, :, None] + NA_COLS)
    d_col = jnp.clip(kc - qcol[:, :, None] + NA_COLS - 1, 0, 2 * NA_COLS - 2)
    d_row = row_idx - rows[:, None] + NA_ROWS - 1
    bias = rpb[:, d_row[:, None, None, :, None], d_col[None, :, :, None, :]]
    qg = q.reshape(b, hh, rows_n, ncb, NA_COLS, d)
    gi_r, gi_c = row_idx[:, :, None, None], col_idx[None, None, :, :]
    kg = k.reshape(b, hh, rows_n, GRID_W, d)[:, :, gi_r, gi_c]
    vg = v.reshape(b, hh, rows_n, GRID_W, d)[:, :, gi_r, gi_c]
    s_loc = jnp.einsum('bhrcqd,bhrkcwd->bhrcqkw', qg, kg).astype(jnp.float32) + bias.astype(jnp.float32)
    s_loc = jnp.where(in_win[:, :, None, :], s_loc, NEG_INF)
    s_ctx = jnp.einsum('bhrcqd,bhld->bhrcql', qg, ck).astype(jnp.float32)
    n_loc = kr * kbw
    s = jnp.concatenate([s_loc.reshape(b, hh, rows_n, ncb, NA_COLS, n_loc), s_ctx], axis=-1)
    pr = jax.nn.softmax(s, axis=-1).astype(v.dtype)
    p_loc = pr[..., :n_loc].reshape(b, hh, rows_n, ncb, NA_COLS, kr, kbw)
    out = (jnp.einsum('bhrcqkw,bhrkcwd->bhrcqd', p_loc, vg)
           + jnp.einsum('bhrcql,bhld->bhrcqd', pr[..., n_loc:], cv))
    return out.reshape(b, hh, t, d)


def context_mixer(h, p, lam_init):
    b = h.shape[0]
    lx, lg, nq, nk, nv, dq, dk, dv = in_projection(h, p['w_in'])
    zeros = jnp.zeros((b, LRU_W), jnp.float32)
    ya, sf, sb = lru_bidir(dwconv_centred(lx, p['lru_conv_w'], p['lru_conv_b']), lg, p, zeros, zeros)
    q, k, v = na_qkv(nq, nk, nv, p)
    yb = blocked_attention(q, k, v)
    q2, k2, v2 = df_qkv(dq, dk, dv, p)
    o = blocked_attention(q2 * (DF_HD ** -0.5), k2, v2[:, :, None])
    yc = diff_combine(o, p, lam_init)
    y = jnp.concatenate([ya, merge_heads(yb), merge_heads(yc)], axis=-1) @ p['w_out']
    return y, (jnp.stack([sf, sb], axis=1), k, v, k2, v2)


def latent_mixer(h, st, ck, cv, dck, dcv, p, lam_init):
    lx, lg, nq, nk, nv, dq, dk, dv = in_projection(h, p['w_in'])
    ya, _, _ = lru_bidir(dwconv_centred(lx, p['lru_conv_w'], p['lru_conv_b']), lg, p, st[:, 0], st[:, 1])
    q, k, v = na_qkv(nq, nk, nv, p)
    yb = neighbourhood_attention(q, k, v, ck, cv, p['na_rpb'])
    q2, k2, v2 = df_qkv(dq, dk, dv, p)
    q2 = rope_2d(q2) * (DF_HD ** -0.5)
    keys = jnp.concatenate([rope_2d(k2), dck.astype(k2.dtype)], axis=-2)
    vals = jnp.concatenate([v2, dcv.astype(v2.dtype)], axis=-2)[:, :, None]
    yc = diff_combine(blocked_attention(q2, keys, vals), p, lam_init)
    y = jnp.concatenate([ya, merge_heads(yb), merge_heads(yc)], axis=-1) @ p['w_out']
    return y, None


def conv_ffn(h, p):
    u = dwconv_centred(h @ p['ffn_up'], p['ffn_conv_w'], p['ffn_conv_b'])
    val, gate = jnp.split(u, 2, axis=-1)
    return (jax.nn.gelu(gate) * val) @ p['ffn_down']


def trunk_layer(x, mod, mixer_fn, p):
    sh_a, sc_a, g_a, sh_f, sc_f, g_f = jnp.split(mod, N_MOD, axis=-1)
    y, extra = mixer_fn(modulate(rms_norm(x, p['norm_mix']), sh_a, sc_a))
    x = x + g_a * y
    x = x + g_f * conv_ffn(modulate(rms_norm(x, p['norm_ffn']), sh_f, sc_f), p)
    return x, extra


def setup_inputs(seed: int = 0) -> dict:
    key = jax.random.key(seed)
    ks = iter(jax.random.split(key, 40))

    def nrm(shape, s):
        return jax.random.normal(next(ks), shape, jnp.float32) * s

    u = jax.random.uniform(next(ks), (DEPTH, 2, LRU_W), jnp.float32, 0.9, 0.999)
    sa = u ** (1.0 / LRU_C)
    return {
        "x_prompt": nrm((BATCH, SEQ, D_MODEL), 1.0),
        "x_sample": nrm((DEC_BATCH, DEC_SEQ, D_MODEL), 1.0),
        "state_lru": nrm((DEC_BATCH, DEPTH, 2, LRU_W), 0.5),
        "cache_na_k": nrm((DEC_BATCH, DEPTH, NA_HEADS, PAST_LEN, NA_HD), 1.0),
        "cache_na_v": nrm((DEC_BATCH, DEPTH, NA_HEADS, PAST_LEN, NA_HD), 1.0),
        "cache_df_k": nrm((DEC_BATCH, DEPTH, DF_HEADS, 2, PAST_LEN, DF_HD), 1.0),
        "cache_df_v": nrm((DEC_BATCH, DEPTH, DF_HEADS, PAST_LEN, DF_VD), 1.0),
        "c": nrm((DEC_BATCH, D_MODEL), 1.0),
        "c_ctx": nrm((D_MODEL,), 1.0),
        "ada_w": nrm((DEPTH, D_MODEL, N_MOD * D_MODEL), D_MODEL ** -0.5),
        "ada_b": nrm((DEPTH, N_MOD * D_MODEL), 0.02),
        "norm_mix": 1.0 + nrm((DEPTH, D_MODEL), 0.05),
        "norm_ffn": 1.0 + nrm((DEPTH, D_MODEL), 0.05),
        "w_in": nrm((DEPTH, D_MODEL, D_IN), D_MODEL ** -0.5),
        "lru_conv_w": nrm((DEPTH, LRU_CONV, LRU_W), LRU_CONV ** -0.5),
        "lru_conv_b": nrm((DEPTH, LRU_W), 0.01),
        "lru_wa": nrm((DEPTH, 2, LRU_BLOCKS, LRU_BW, LRU_BW), LRU_BW ** -0.5),
        "lru_ba": nrm((DEPTH, 2, LRU_W), 0.01),
        "lru_wx": nrm((DEPTH, 2, LRU_BLOCKS, LRU_BW, LRU_BW), LRU_BW ** -0.5),
        "lru_bx": nrm((DEPTH, 2, LRU_W), 0.01),
        "lru_lambda": jnp.log(sa) - jnp.log1p(-sa),
        "na_q_norm": 1.0 + nrm((DEPTH, NA_HD), 0.05),
        "na_k_norm": 1.0 + nrm((DEPTH, NA_HD), 0.05),
        "na_rpb": nrm((DEPTH, NA_HEADS, 2 * NA_ROWS - 1, 2 * NA_COLS - 1), 0.1),
        "df_q_norm": 1.0 + nrm((DEPTH, DF_HD), 0.05),
        "df_k_norm": 1.0 + nrm((DEPTH, DF_HD), 0.05),
        "df_lambda": nrm((DEPTH, 4, DF_HD), 0.1),
        "df_subln": 1.0 + nrm((DEPTH, DF_VD), 0.05),
        "w_out": nrm((DEPTH, D_CAT, D_MODEL), D_CAT ** -0.5),
        "ffn_up": nrm((DEPTH, D_MODEL, 2 * D_FF), D_MODEL ** -0.5),
        "ffn_conv_w": nrm((DEPTH, FFN_CONV, 2 * D_FF), FFN_CONV ** -0.5),
        "ffn_conv_b": nrm((DEPTH, 2 * D_FF), 0.01),
        "ffn_down": nrm((DEPTH, D_FF, D_MODEL), D_FF ** -0.5),
    }


def reference(x_prompt, x_sample, state_lru, cache_na_k, cache_na_v, cache_df_k, cache_df_v, c, c_ctx,
              ada_w, ada_b, norm_mix, norm_ffn, w_in, lru_conv_w, lru_conv_b, lru_wa, lru_ba, lru_wx, lru_bx,
              lru_lambda, na_q_norm, na_k_norm, na_rpb, df_q_norm, df_k_norm, df_lambda, df_subln, w_out,
              ffn_up, ffn_conv_w, ffn_conv_b, ffn_down):
    y_p = x_prompt
    y_s = x_sample
    s_lru, s_nk, s_nv, s_dk, s_dv = [], [], [], [], []
    for l in range(DEPTH):
        p = dict(w_in=w_in[l], lru_conv_w=lru_conv_w[l], lru_conv_b=lru_conv_b[l], lru_wa=lru_wa[l],
                 lru_ba=lru_ba[l], lru_wx=lru_wx[l], lru_bx=lru_bx[l], lru_lambda=lru_lambda[l],
                 na_q_norm=na_q_norm[l], na_k_norm=na_k_norm[l], na_rpb=na_rpb[l], df_q_norm=df_q_norm[l],
                 df_k_norm=df_k_norm[l], df_lambda=df_lambda[l], df_subln=df_subln[l], w_out=w_out[l],
                 norm_mix=norm_mix[l], norm_ffn=norm_ffn[l], ffn_up=ffn_up[l], ffn_conv_w=ffn_conv_w[l],
                 ffn_conv_b=ffn_conv_b[l], ffn_down=ffn_down[l])
        lam_init = 0.8 - 0.6 * math.exp(-0.3 * l)
        mod_ctx = (jax.nn.silu(c_ctx) @ ada_w[l] + ada_b[l])[None, None, :]
        y_p, (st, nk, nv, dk, dv) = trunk_layer(y_p, mod_ctx, lambda h: context_mixer(h, p, lam_init), p)
        s_lru.append(st)
        s_nk.append(nk)
        s_nv.append(nv)
        s_dk.append(dk)
        s_dv.append(dv)
        mod_lat = (jax.nn.silu(c) @ ada_w[l] + ada_b[l])[:, None, :]
        y_s, _ = trunk_layer(y_s, mod_lat,
                             lambda h: latent_mixer(h, state_lru[:, l], cache_na_k[:, l], cache_na_v[:, l],
                                                    cache_df_k[:, l], cache_df_v[:, l], p, lam_init), p)
    new_state_lru = jnp.stack(s_lru, axis=1)
    new_na_k = jnp.stack(s_nk, axis=1)
    new_na_v = jnp.stack(s_nv, axis=1)
    new_df_k = jnp.stack(s_dk, axis=1)
    new_df_v = jnp.stack(s_dv, axis=1)
    return (y_p, y_s, new_state_lru, new_na_k, new_na_v, new_df_k, new_df_v)
```

```python
import contextlib
import numpy as np
import concourse.bass as bass
import concourse.mybir as mybir
from concourse.bass_utils import run_bass_kernel_spmd

F32 = mybir.dt.float32
BF16 = mybir.dt.bfloat16
AF = mybir.ActivationFunctionType
ALU = mybir.AluOpType
AX = mybir.AxisListType

ENGS = ("sp", "act", "pe", "dve", "pool")

D = 1024
NP_T = 1024
NS_T = 4096
TT = NP_T + NS_T
NTILE = TT // 512
NSUB = TT // 128
L_CTX = 512
D_IN = 2560
D_FF = 2816
EPS = 1e-6
SEQS = [(0, 256), (256, 256), (512, 256), (768, 256), (1024, 4096)]


class Buf:
    __slots__ = ("name", "lws", "rd", "rd_dma", "multi")

    def __init__(self, name="", multi=False):
        self.name = name
        self.lws = []
        self.rd = {}
        self.rd_dma = []
        self.multi = multi


def MB(name=""):
    return Buf(name, multi=True)


class Op:
    __slots__ = ("eng", "fn", "is_dma", "deps", "signal", "ticket", "sem", "semval", "semprev", "idx")

    def __init__(self, eng, fn, is_dma):
        self.eng = eng
        self.fn = fn
        self.is_dma = is_dma
        self.deps = []
        self.signal = False
        self.ticket = None
        self.sem = None
        self.semval = None
        self.semprev = None


class Sched:
    def __init__(self, nc, n_dma_sems=16):
        self.nc = nc
        self.ops = []
        self.n_dma_sems = n_dma_sems
        self.last = {}
        self.recent_dma = {e: [] for e in ENGS}

    def _add(self, eng, fn, r, w, is_dma):
        op = Op(eng, fn, is_dma)
        op.idx = len(self.ops)
        deps = {}
        for b in r:
            for y in b.lws:
                deps[y.idx] = y
        for b in w:
            if not b.multi:
                for y in b.lws:
                    deps[y.idx] = y
            for e, y in b.rd.items():
                if y.eng == eng and not is_dma:
                    continue
                deps[y.idx] = y
            for y in b.rd_dma:
                deps[y.idx] = y
        for y in deps.values():
            if (not y.is_dma) and y.eng == "pe" and eng == "pe" and not is_dma:
                continue
            op.deps.append(y)
            y.signal = True
        for b in r:
            if is_dma:
                b.rd_dma.append(op)
            else:
                b.rd[eng] = op
        for b in w:
            if b.multi and not b.rd and not b.rd_dma:
                b.lws.append(op)
            else:
                b.lws = [op]
                b.rd = {}
                b.rd_dma = []
        self.ops.append(op)
        if is_dma:
            self.recent_dma[eng].append(op)
            if len(self.recent_dma[eng]) > self.n_dma_sems:
                self.recent_dma[eng].pop(0)
        else:
            self.last[eng] = op
        return op

    def barrier(self):
        B = Op("sp", lambda e: e.nop(), False)
        B.idx = len(self.ops)
        for y in self.last.values():
            if y.eng != "sp":
                B.deps.append(y)
                y.signal = True
        for q in self.recent_dma.values():
            B.deps.extend(q)
        B.signal = True
        self.ops.append(B)
        self.last["sp"] = B
        for e in ENGS:
            if e == "sp":
                continue
            o = Op(e, lambda en: en.nop(), False)
            o.idx = len(self.ops)
            o.deps.append(B)
            self.ops.append(o)
            self.last[e] = o

    def op(self, eng, fn, r=(), w=()):
        return self._add(eng, fn, r, w, False)

    def dma(self, eng, fn, r=(), w=()):
        return self._add(eng, fn, r, w, True)

    def emit(self, final_wait_eng="sp"):
        nc = self.nc
        with contextlib.ExitStack() as st:
            esem = {e: st.enter_context(nc.semaphore("es_" + e)) for e in ENGS}
            dsem = {e: [st.enter_context(nc.semaphore("ds_%s%d" % (e, i))) for i in range(self.n_dma_sems)]
                    for e in ("sp", "act", "pool")}
            cnt = {e: 0 for e in ENGS}
            dcnt = {e: 0 for e in ENGS}
            for op in self.ops:
                if op.is_dma:
                    k = dcnt[op.eng]
                    dcnt[op.eng] += 1
                    op.sem = dsem[op.eng][k % self.n_dma_sems]
                    op.semprev = 16 * (k // self.n_dma_sems)
                    op.semval = op.semprev + 16
                elif op.signal:
                    cnt[op.eng] += 1
                    op.ticket = cnt[op.eng]
            final = []
            for e in ("sp", "act", "pool"):
                n = dcnt[e]
                for i in range(min(n, self.n_dma_sems)):
                    last_k = ((n - 1 - i) // self.n_dma_sems) * self.n_dma_sems + i
                    final.append((dsem[e][i], 16 * (last_k // self.n_dma_sems) + 16))
            ops = self.ops
            blk = st.enter_context(nc.Block())

            def run(engname, eng):
                waited = {}
                for op in ops:
                    if op.eng != engname:
                        continue
                    for y in op.deps:
                        if y.is_dma:
                            key, val, sem = ("d", id(y.sem)), y.semval, y.sem
                        else:
                            key, val, sem = ("e", y.eng), y.ticket, esem[y.eng]
                        if waited.get(key, 0) >= val:
                            continue
                        waited[key] = val
                        eng.wait_ge(sem, val)
                    if op.is_dma:
                        key = ("d", id(op.sem))
                        if op.semprev > 0 and waited.get(key, 0) < op.semprev:
                            eng.wait_ge(op.sem, op.semprev)
                            waited[key] = op.semprev
                        ins = op.fn(eng)
                        ins.then_inc(op.sem, 16)
                    else:
                        ins = op.fn(eng)
                        if op.signal:
                            ins.then_inc(esem[engname], 1)
                if engname == final_wait_eng:
                    for sem, val in final:
                        eng.wait_ge(sem, val)
                    for e in ENGS:
                        if cnt[e] > 0:
                            eng.wait_ge(esem[e], cnt[e])

            @blk.sync
            def _(e):
                run("sp", e)

            @blk.scalar
            def _(e):
                run("act", e)

            @blk.tensor
            def _(e):
                run("pe", e)

            @blk.vector
            def _(e):
                run("dve", e)

            @blk.gpsimd
            def _(e):
                run("pool", e)


def I(name, *a, **k):
    return lambda e: getattr(e, name)(*a, **k)


def ap_of(t):
    return t[:] if not isinstance(t, bass.AP) else t


def mkap(base, off, dims):
    return bass.AP(base.tensor, base.offset + off, [list(d) for d in dims])


INPUT_SPECS = [
    ("xin", [TT, D]), ("cvec", [2, D]), ("state", [2, 2, 512]),
    ("cnk", [2, 4, 512, 64]), ("cnv", [2, 4, 512, 64]), ("cdk", [2, 4, 2, 512, 32]), ("cdv", [2, 4, 512, 64]),
    ("ada_w", [2, D, 6 * D]), ("ada_b", [2, 6 * D]), ("norm_mix", [2, D]), ("norm_ffn", [2, D]),
    ("w_in", [2, D, D_IN]), ("lru_conv_w", [2, 4, 512]), ("lru_conv_b", [2, 512]),
    ("lru_wa", [2, 2, 8, 64, 64]), ("lru_ba", [2, 2, 512]), ("lru_wx", [2, 2, 8, 64, 64]), ("lru_bx", [2, 2, 512]),
    ("lru_lambda", [2, 2, 512]), ("na_q_norm", [2, 64]), ("na_k_norm", [2, 64]), ("na_rpb", [2, 4, 15, 31]),
    ("df_q_norm", [2, 32]), ("df_k_norm", [2, 32]), ("df_lambda", [2, 4, 32]), ("df_subln", [2, 64]),
    ("w_out", [2, D, D]), ("ffn_up", [2, D, 2 * D_FF]), ("ffn_conv_w", [2, 3, 2 * D_FF]), ("ffn_conv_b", [2, 2 * D_FF]),
    ("ffn_down", [2, D_FF, D]),
    ("c_ident", [128, 128]), ("c_p60", [60, 60]), ("c_oh", [32, 4096]), ("c_ropec", [NS_T, 32]), ("c_ropes", [NS_T, 32]),
]
OUTPUT_SPECS = [
    ("y", [TT, D]), ("o_lru", [4, 2, 2, 512]), ("o_nk", [4, 2, 4, 256, 64]), ("o_nv", [4, 2, 4, 256, 64]),
    ("o_dk", [4, 2, 4, 2, 256, 32]), ("o_dv", [4, 2, 4, 256, 64]),
]
SCRATCH_SPECS = [
    ("MOD", [2, 2, 6 * D], F32), ("LX", [512, TT], F32), ("LG", [512, TT], BF16),
    ("NQT", [256, TT], BF16), ("NKT", [256, TT + L_CTX], BF16), ("NV", [TT + L_CTX, 256], BF16),
    ("DQT", [256, TT], BF16), ("DKT", [256, TT + L_CTX], BF16), ("DV", [TT + L_CTX, 256], BF16),
    ("CATT", [D, TT], BF16), ("XMID", [TT, D], F32), ("ACTT", [D_FF, TT], BF16), ("XRES", [TT, D], F32),
    ("ZR", [2, 60, 4096], BF16),
]


def build(stage=99, debug=False):
    nc = bass.Bass("TRN2", target_bir_lowering=False)
    din = {n: nc.dram_tensor(n, s, F32, kind="ExternalInput").ap() for n, s in INPUT_SPECS}
    dout = {n: nc.dram_tensor(n, s, F32, kind="ExternalOutput").ap() for n, s in OUTPUT_SPECS}
    scr = {n: nc.dram_tensor(n, s, dt, kind=("ExternalOutput" if debug else "Internal")).ap()
           for n, s, dt in SCRATCH_SPECS}
    with contextlib.ExitStack() as st:
        st.enter_context(nc.allow_non_contiguous_dma(reason="small strided parameter loads"))
        st.enter_context(nc.allow_low_precision(reason="bf16 matmul operands, fp32 accumulation"))
        S = Sched(nc)
        _emit_program(nc, st, S, din, dout, scr, stage)
        S.emit()
    return nc


def _emit_program(nc, st, S, din, dout, scr, stage):
    uid = [0]
    cur = [st]

    def sb(shape, dt=F32, name="t"):
        uid[0] += 1
        return cur[0].enter_context(nc.sbuf_tensor("%s_%d" % (name, uid[0]), shape, dt))

    @contextlib.contextmanager
    def phase():
        prev = cur[0]
        with contextlib.ExitStack() as s2:
            cur[0] = s2
            try:
                yield
            finally:
                cur[0] = prev
                S.barrier()
                ps_banks[0] = list(range(8))

    psum = [st.enter_context(nc.psum_tensor("ps%d" % i, [128, 512], F32)) for i in range(8)]
    psb = [Buf("ps%d" % i) for i in range(8)]
    psi = [0]

    ps_banks = [list(range(8))]

    def PS():
        bl = ps_banks[0]
        i = bl[psi[0] % len(bl)]
        psi[0] += 1
        return psum[i], psb[i]

    def pst(t):
        return ap_of(t).ap[0][0]

    ident = sb([128, 128], BF16, "ident")
    b_ident = Buf("ident")
    ones64 = sb([64, 64], F32, "ones64")
    b_ones64 = Buf()
    sel65 = sb([65, 64], F32, "sel65")
    b_sel65 = Buf()
    eps_t = sb([128, 1], F32, "eps")
    b_eps = Buf()
    with phase():
        ident_f = sb([128, 128], F32, "identf")
        bt = Buf()
        S.dma("sp", I("dma_start", out=ident_f[:], in_=din["c_ident"]), w=[bt])
        S.op("dve", I("tensor_copy", out=ident[:], in_=ident_f[:]), r=[bt], w=[b_ident])
    S.op("pool", I("memset", ones64[:], 1.0), w=[b_ones64])
    S.op("pool", I("memset", sel65[:], 0.0), w=[b_sel65])
    S.op("pool", I("memset", sel65[64:65, :], 1.0), w=[b_sel65])
    S.op("pool", I("memset", eps_t[:], EPS), w=[b_eps])

    def bcast_row(dram_row_ap, n, parts=128):
        return bass.AP(dram_row_ap.tensor, dram_row_ap.offset, [[0, parts], [1, n]])

    bMOD = MB("MOD")
    bLX = [MB() for _ in range(NTILE)]
    bLG = [MB() for _ in range(NTILE)]
    bNQT = [MB() for _ in range(NSUB)]
    bNKT = [MB() for _ in range(NSUB + 4)]
    bNV = [MB() for _ in range(NSUB + 4)]
    bDQT = [MB() for _ in range(NSUB)]
    bDKT = [MB() for _ in range(NSUB + 4)]
    bDV = [MB() for _ in range(NSUB + 4)]
    bCATT = [[MB() for _ in range(NTILE)] for _ in range(16)]
    bXMID = [MB() for _ in range(NSUB)]
    bACTT = [[MB() for _ in range(NTILE)] for _ in range(22)]
    bXRES = [MB() for _ in range(NSUB)]
    bZR = MB("ZRunused")

    LP = []
    for _l in range(2):
        LP.append(dict(prm=sb([128, 4, 11], F32, "prm"), b_prm=MB(), h0t=sb([128, 4, 2], F32, "h0t"), b_h0=MB(),
                       c8=sb([128, 4, 2], F32, "c8"), b_c8=Buf(), c8b=sb([128, 4, 2], F32, "c8b"), wbd=sb([128, 16, 128], BF16, "wbd"), b_wbd=Buf(),
                       nlam=sb([128, 1], F32, "nlam"), b_nlam=Buf(), gsub=sb([64, 1], F32, "gsub"), b_gsub=Buf()))
    bZRl = [MB("ZR0"), MB("ZR1")]

    def prep_layer(l, lam_init):
        P_ = LP[l]
        prm, b_prm, h0t, b_h0, c8, b_c8, wbd, b_wbd = P_["prm"], P_["b_prm"], P_["h0t"], P_["b_h0"], P_["c8"], P_["b_c8"], P_["wbd"], P_["b_wbd"]
        pa = prm[:]
        def pdst(off, n):
            return mkap(pa, off, [[pst(prm), 128], [11, 4], [1, n]])
        for c in range(4):
            src = din["lru_conv_w"]
            S.dma("sp", I("dma_start", out=prm[:, c, 0:4], in_=bass.AP(src.tensor, src.offset + l * 4 * 512 + c * 128, [[1, 128], [512, 4]])), w=[b_prm])
            src = din["lru_conv_b"]
            S.dma("sp", I("dma_start", out=prm[:, c, 4:5], in_=bass.AP(src.tensor, src.offset + l * 512 + c * 128, [[1, 128], [1, 1]])), w=[b_prm])
            for nm, off in (("lru_ba", 5), ("lru_bx", 7), ("lru_lambda", 9)):
                src = din[nm]
                S.dma("sp", I("dma_start", out=prm[:, c, off:off + 2], in_=bass.AP(src.tensor, src.offset + l * 1024 + c * 128, [[1, 128], [512, 2]])), w=[b_prm])
            src = din["state"]
            S.dma("sp", I("dma_start", out=h0t[:, c, :], in_=bass.AP(src.tensor, src.offset + l * 1024 + c * 128, [[1, 128], [512, 2]])), w=[b_h0])
        za = sb([128, 8], F32, "za"); zw = sb([128, 8], F32, "zw"); zs = sb([128, 8], F32, "zs"); z2 = sb([128, 8], F32, "z2"); zp = sb([128, 8], F32, "zp")
        b_z = Buf()
        lam_ap = mkap(pa, 9, [[pst(prm), 128], [11, 4], [1, 2]])
        za3 = za[:].rearrange("p (c d) -> p c d", d=2)
        S.op("dve", I("tensor_scalar", out=za3, in0=lam_ap, scalar1=-1.0, scalar2=None, op0=ALU.mult), r=[b_prm], w=[b_z])
        S.op("dve", I("tensor_tensor", out=zw[:].rearrange("p (c d) -> p c d", d=2), in0=za3, in1=lam_ap, op=ALU.max), r=[b_z, b_prm], w=[b_z])
        S.op("act", I("activation", out=zw[:], in_=zw[:], func=AF.Exp, scale=-1.0), r=[b_z], w=[b_z])
        S.op("dve", I("tensor_scalar", out=zs[:], in0=zw[:], scalar1=2.0, scalar2=None, op0=ALU.add), r=[b_z], w=[b_z])
        S.op("dve", I("reciprocal", out=zs[:], in_=zs[:]), r=[b_z], w=[b_z])
        S.op("dve", I("tensor_tensor", out=zs[:], in0=zs[:], in1=zw[:], op=ALU.mult), r=[b_z], w=[b_z])
        S.op("dve", I("tensor_tensor", out=z2[:], in0=zs[:], in1=zs[:], op=ALU.mult), r=[b_z], w=[b_z])
        S.op("dve", I("tensor_scalar", out=zp[:], in0=z2[:], scalar1=1.0 / 9, scalar2=1.0 / 7, op0=ALU.mult, op1=ALU.add), r=[b_z], w=[b_z])
        for cf in (1.0 / 5, 1.0 / 3, 1.0):
            S.op("dve", I("tensor_tensor", out=zp[:], in0=zp[:], in1=z2[:], op=ALU.mult), r=[b_z], w=[b_z])
            S.op("dve", I("tensor_scalar", out=zp[:], in0=zp[:], scalar1=cf, scalar2=None, op0=ALU.add), r=[b_z], w=[b_z])
        S.op("dve", I("tensor_tensor", out=zp[:], in0=zp[:], in1=zs[:], op=ALU.mult), r=[b_z], w=[b_z])
        S.op("dve", I("tensor_scalar", out=za[:], in0=za[:], scalar1=0.0, scalar2=0.5, op0=ALU.max, op1=ALU.mult), r=[b_z], w=[b_z])
        S.op("dve", I("tensor_tensor", out=zp[:], in0=zp[:], in1=za[:], op=ALU.add), r=[b_z], w=[b_z])
        S.op("dve", I("tensor_scalar", out=c8[:].rearrange("p c d -> p (c d)"), in0=zp[:], scalar1=-16.0, scalar2=None, op0=ALU.mult), r=[b_z], w=[b_c8])
        S.op("dve", I("tensor_scalar", out=P_["c8b"][:].rearrange("p c d -> p (c d)"), in0=zp[:], scalar1=-32.0, scalar2=None, op0=ALU.mult), r=[b_z], w=[b_c8])
        wbd_f = sb([128, 16, 128], F32, "wbdf"); b_wbdf = MB(); b_wbdz = Buf()
        S.op("pool", I("memset", wbd_f[:], 0.0), w=[b_wbdf, b_wbdz])
        for c in range(4):
            for dr in range(2):
                for gate, nm in enumerate(("lru_wa", "lru_wx")):
                    gi = (c * 2 + dr) * 2 + gate
                    for half in range(2):
                        S.dma("sp", I("dma_start", out=wbd_f[half * 64:(half + 1) * 64, gi, half * 64:(half + 1) * 64], in_=din[nm][l, dr, 2 * c + half]), r=[b_wbdz], w=[b_wbdf])
        S.op("dve", I("tensor_copy", out=wbd[:], in_=wbd_f[:]), r=[b_wbdf], w=[b_wbd])
        dl = sb([128, 4, 32], F32, "dl"); b_dl = Buf()
        S.dma("sp", I("dma_start", out=dl[:].rearrange("p a b -> p (a b)"), in_=bass.AP(din["df_lambda"].tensor, din["df_lambda"].offset + l * 128, [[0, 128], [1, 128]])), w=[b_dl])
        pr = sb([128, 2, 32], F32, "pr"); b_pr = Buf()
        d4 = dl[:].rearrange("p (a two) b -> p a two b", two=2)
        S.op("dve", I("tensor_tensor", out=pr[:], in0=d4[:, :, 0, :], in1=d4[:, :, 1, :], op=ALU.mult), r=[b_dl], w=[b_pr])
        sm = sb([128, 2], F32, "sm"); b_sm = Buf()
        S.op("dve", I("tensor_reduce", out=sm[:], in_=pr[:], axis=AX.X, op=ALU.add), r=[b_pr], w=[b_sm])
        S.op("act", I("activation", out=sm[:], in_=sm[:], func=AF.Exp), r=[b_sm], w=[b_sm])
        nlam, b_nlam, gsub, b_gsub = LP[l]["nlam"], LP[l]["b_nlam"], LP[l]["gsub"], LP[l]["b_gsub"]
        bZR = bZRl[l]
        S.op("dve", I("tensor_tensor", out=nlam[:], in0=sm[:, 1:2], in1=sm[:, 0:1], op=ALU.subtract), r=[b_sm], w=[b_nlam])
        S.op("dve", I("tensor_scalar", out=nlam[:], in0=nlam[:], scalar1=-lam_init, scalar2=None, op0=ALU.add), r=[b_nlam], w=[b_nlam])
        src = din["df_subln"]
        S.dma("sp", I("dma_start", out=gsub[:], in_=bass.AP(src.tensor, src.offset + l * 64, [[1, 64], [1, 1]])), w=[b_gsub])
        S.op("dve", I("tensor_scalar", out=gsub[:], in0=gsub[:], scalar1=1.0 - lam_init, scalar2=None, op0=ALU.mult), r=[b_gsub], w=[b_gsub])
        if True:
            rpa = sb([60, 32], F32, "rpa"); b_rpa = Buf()
            S.op("pool", I("memset", rpa[:], 1.0), w=[b_rpa])
            S.dma("sp", I("dma_start", out=rpa[:, 0:31], in_=din["na_rpb"][l].rearrange("h r c -> (h r) c")), w=[b_rpa])
            p60 = sb([60, 60], F32, "p60"); b_p60 = Buf()
            S.dma("sp", I("dma_start", out=p60[:], in_=din["c_p60"]), w=[b_p60])
            oh = sb([32, 4096], F32, "oh"); b_oh = Buf()
            S.dma("sp", I("dma_start", out=oh[:], in_=din["c_oh"]), w=[b_oh])
            ps, pb = PS()
            S.op("pe", I("matmul", ps[0:32, 0:60], lhsT=rpa[:], rhs=p60[:], start=True, stop=True), r=[b_rpa, b_p60], w=[pb])
            l2 = sb([32, 60], F32, "l2"); b_l2 = Buf()
            S.op("act", I("activation", out=l2[:], in_=ps[0:32, 0:60], func=AF.Copy), r=[pb], w=[b_l2])
            zs = [sb([60, 512], BF16, "zs") for _ in range(2)]; b_zs = [Buf(), Buf()]
            for nb in range(8):
                ps, pb = PS()
                S.op("pe", I("matmul", ps[0:60, :], lhsT=l2[:], rhs=oh[:, nb * 512:(nb + 1) * 512], start=True, stop=True), r=[b_l2, b_oh], w=[pb])
                S.op("act", I("activation", out=zs[nb % 2][:], in_=ps[0:60, :], func=AF.Exp), r=[pb], w=[b_zs[nb % 2]])
                S.dma("sp", I("dma_start", out=scr["ZR"][l][:, nb * 512:(nb + 1) * 512], in_=zs[nb % 2][:]), r=[b_zs[nb % 2]], w=[bZR])


    with phase():
        for _l in range(2):
            prep_layer(_l, 0.8 - 0.6 * float(np.exp(-0.3 * _l)))
        cT = sb([128, 8, 2], F32, "cT")
        b_cT = Buf()
        for g in range(2):
            S.dma("sp", I("dma_start", out=cT[:, :, g], in_=din["cvec"][g].rearrange("(k p) -> p k", p=128)), w=[b_cT])
        S.op("act", I("activation", out=cT[:], in_=cT[:], func=AF.Silu), r=[b_cT], w=[b_cT])
        adab = sb([2, 2, 6 * D], F32, "adab")
        modsb = [sb([2, 512], F32, "modsb") for _ in range(2)]
        aw = [sb([128, 8, 512], F32, "aw%d" % i) for i in range(3)]
        b_aw = [Buf(), Buf(), Buf()]
        b_adab = MB()
        b_modsb = [Buf(), Buf()]
        for l in range(2):
            S.dma("sp", I("dma_start", out=adab[:, l, :], in_=bass.AP(din["ada_b"].tensor, din["ada_b"].offset + l * 6 * D, [[0, 2], [1, 6 * D]])), w=[b_adab])
        def aw_load(it):
            l_, nb_ = it // 12, it % 12
            S.dma("sp", I("dma_start", out=aw[it % 3][:], in_=din["ada_w"][l_, :, nb_ * 512:(nb_ + 1) * 512].rearrange("(k p) n -> p k n", p=128)), w=[b_aw[it % 3]])

        for it in range(3):
            aw_load(it)
        for it in range(24):
            l, nb = it // 12, it % 12
            k = it % 3
            ps, pb = PS()
            for kc in range(8):
                S.op("pe", I("matmul", ps[0:2, :], lhsT=cT[:, kc, :], rhs=aw[k][:, kc, :], start=(kc == 0), stop=(kc == 7)),
                     r=[b_cT, b_aw[k]], w=[pb])
            m_ = it % 2
            S.op("dve", I("tensor_tensor", out=modsb[m_][:], in0=ps[0:2, :], in1=adab[:, l, nb * 512:(nb + 1) * 512], op=ALU.add), r=[pb, b_adab], w=[b_modsb[m_]])
            S.dma("sp", I("dma_start", out=scr["MOD"][l, :, nb * 512:(nb + 1) * 512], in_=modsb[m_][:]), r=[b_modsb[m_]], w=[bMOD])
            if it + 3 < 24:
                aw_load(it + 3)
    if stage <= 0:
        return

    mpool = [sb([128, D], F32, "mod%d" % i) for i in range(6)]
    mpool_b = [Buf() for _ in range(6)]
    gtmp = sb([128, D], F32, "gtmp")
    b_gtmp = Buf()

    def mod_row(l, g, j):
        m = scr["MOD"]
        return bass.AP(m.tensor, m.offset + ((l * 2 + g) * 6 + j) * D, [[0, 128], [1, D]])

    def load_mod(l, g, j, slot):
        S.dma("sp", I("dma_start", out=mpool[slot][:], in_=mod_row(l, g, j)), r=[bMOD], w=[mpool_b[slot]])

    def load_gs(l, g, j_scale, norm_name, slot):
        nrm = din[norm_name]
        S.dma("sp", I("dma_start", out=gtmp[:], in_=bass.AP(nrm.tensor, nrm.offset + l * D, [[0, 128], [1, D]])), w=[b_gtmp])
        load_mod(l, g, j_scale, slot)
        S.op("dve", I("scalar_tensor_tensor", out=mpool[slot][:], in0=mpool[slot][:], scalar=1.0, in1=gtmp[:], op0=ALU.add, op1=ALU.mult),
             r=[mpool_b[slot], b_gtmp], w=[mpool_b[slot]])

    def mk_wk():
        return dict(sq=sb([128, D], BF16, "sq"), ss=sb([128, 1], F32, "ss"), t1=sb([128, D], F32, "t1"), hb=sb([128, D], BF16, "hb"),
                    b_sq=Buf(), b_ss=Buf(), b_t1=Buf(), b_hb=Buf())

    def pipeline(n, stages, skew=1):
        ns = len(stages)
        for t in range(n + (ns - 1) * skew):
            for si, f in enumerate(stages):
                i = t - si * skew
                if 0 <= i < n:
                    f(i)

    def rms_mod_transpose(xt, bx, j, GS, SH, bGS, bSH, wk, hT, b_hT):
        rms_mod(xt, bx, GS, SH, bGS, bSH, wk)
        transpose_to_hT(j, wk, hT, b_hT)

    def rms_mod(xt, bx, GS, SH, bGS, bSH, wk):
        sq, ss, t1, hb = wk["sq"], wk["ss"], wk["t1"], wk["hb"]
        S.op("act", I("activation", out=sq[:], in_=xt[:], func=AF.Square, accum_out=ss[:]), r=[bx], w=[wk["b_sq"], wk["b_ss"]])
        S.op("act", I("activation", out=ss[:], in_=ss[:], func=AF.Sqrt, scale=1.0 / D, bias=eps_t[:, 0:1]), r=[wk["b_ss"], b_eps], w=[wk["b_ss"]])
        S.op("dve", I("reciprocal", out=ss[:], in_=ss[:]), r=[wk["b_ss"]], w=[wk["b_ss"]])
        S.op("dve", I("scalar_tensor_tensor", out=t1[:], in0=xt[:], scalar=ss[:, 0:1], in1=GS[:], op0=ALU.mult, op1=ALU.mult),
             r=[bx, wk["b_ss"], bGS], w=[wk["b_t1"]])
        S.op("dve", I("tensor_tensor", out=hb[:], in0=t1[:], in1=SH[:], op=ALU.add), r=[wk["b_t1"], bSH], w=[wk["b_hb"]])

    def transpose_to_hT(j, wk, hT, b_hT):
        hb = wk["hb"]
        for half in range(2):
            ps, pb = PS()
            for q in range(4):
                c = half * 4 + q
                S.op("pe", I("matmul", ps[:, q * 128:(q + 1) * 128], lhsT=hb[:, c * 128:(c + 1) * 128], rhs=ident[:], start=True, stop=True),
                     r=[wk["b_hb"], b_ident], w=[pb])
            dst = hT[:, half * 4:(half + 1) * 4, j * 128:(j + 1) * 128]
            src = ps[:].rearrange("p (q n) -> p q n", q=4)
            S.op("act", I("activation", out=dst, in_=src, func=AF.Copy), r=[pb], w=[b_hT[j]])

    wcnt = [0]

    def p1_phase(l, x_src, bx_src):
        hT = sb([128, 8, TT], BF16, "hT")
        b_hT = [Buf() for _ in range(NSUB)]
        wks = [mk_wk(), mk_wk()]
        xts = [sb([128, D], F32, "xt") for _ in range(2)]
        bxts = [Buf() for _ in range(2)]
        NSET = 3
        tsets = []
        for _i in range(NSET):
            tsets.append((sb([128, 512], F32, "qk"), Buf(), sb([128, 512], F32, "sq2"), Buf(), sb([128, 16], F32, "ss8"), Buf(),
                          sb([128, 512], F32, "n1"), Buf(), sb([128, 512], F32, "n2"), Buf(), None, None,
                          sb([128, 512], BF16, "nb16"), Buf()))
        qk, b_qk, sq2, b_sq2, ss8, b_ss8, n1, b_n1, n2, b_n2, n3, b_n3, nb16, b_nb16 = tsets[0]
        rotc = [0]

        def rot():
            nonlocal qk, b_qk, sq2, b_sq2, ss8, b_ss8, n1, b_n1, n2, b_n2, n3, b_n3, nb16, b_nb16
            rotc[0] += 1
            qk, b_qk, sq2, b_sq2, ss8, b_ss8, n1, b_n1, n2, b_n2, n3, b_n3, nb16, b_nb16 = tsets[rotc[0] % NSET]
        tb16 = [sb([128, 512], BF16, "tb16") for _ in range(2)]; b_tb16 = [Buf(), Buf()]
        vb16 = [sb([128, 256], BF16, "vb16") for _ in range(2)]; b_vb16 = [Buf(), Buf()]
        vf32 = [sb([128, 256], F32, "vf32") for _ in range(2)]; b_vf32 = [Buf(), Buf()]
        stg_f = [sb([128, 512], F32, "stgf") for _ in range(2)]; b_stg_f = [Buf(), Buf()]
        stg_b = [sb([128, 512], BF16, "stgb") for _ in range(2)]; b_stg_b = [Buf(), Buf()]
        gqk = sb([128, 512], F32, "gqk"); b_gqk = MB()
        gdq = sb([128, 256], F32, "gdq"); b_gdq = MB()
        gdk = sb([128, 256], F32, "gdk"); b_gdk = MB()
        ropeC = sb([128, 32, 32], F32, "ropeC"); ropeS = sb([128, 32, 32], F32, "ropeS"); b_rope = MB()
        S.dma("sp", I("dma_start", out=ropeC[:], in_=din["c_ropec"].rearrange("(s p) f -> p s f", p=128)), w=[b_rope])
        S.dma("sp", I("dma_start", out=ropeS[:], in_=din["c_ropes"].rearrange("(s p) f -> p s f", p=128)), w=[b_rope])
        wcg = [sb([128, 8, 512], BF16, "wcg") for _ in range(2)]; b_wcg = [Buf(), Buf()]

        def load_w_cols(dram_w2d, c0, ncols):
            k = wcnt[0] % 2
            wcnt[0] += 1
            S.dma("pool", I("dma_start", out=wcg[k][:, :, 0:ncols], in_=dram_w2d[:, c0:c0 + ncols].rearrange("(k p) n -> p k n", p=128)), w=[b_wcg[k]])
            return wcg[k], b_wcg[k]

        def headnorm(src_ap, src_b, ngrp, gsz, gain_ap, gain_b, out_t, out_b):
            n = ngrp * gsz
            S.op("dve", I("tensor_tensor", out=sq2[:, 0:n], in0=src_ap, in1=src_ap, op=ALU.mult), r=[src_b], w=[b_sq2])
            S.op("dve", I("tensor_reduce", out=ss8[:, 0:ngrp], in_=sq2[:, 0:n].rearrange("p (g d) -> p g d", d=gsz), axis=AX.X, op=ALU.add),
                 r=[b_sq2], w=[b_ss8])
            S.op("act", I("activation", out=ss8[:, 0:ngrp], in_=ss8[:, 0:ngrp], func=AF.Sqrt, scale=1.0 / gsz, bias=eps_t[:, 0:1]), r=[b_ss8, b_eps], w=[b_ss8])
            S.op("dve", I("reciprocal", out=ss8[:, 0:ngrp], in_=ss8[:, 0:ngrp]), r=[b_ss8], w=[b_ss8])
            rb = mkap(ss8[:], 0, [[pst(ss8), 128], [1, ngrp], [0, gsz]])
            S.op("dve", I("tensor_tensor", out=n1[:, 0:n].rearrange("p (g d) -> p g d", d=gsz), in0=src_ap.rearrange("p (g d) -> p g d", d=gsz), in1=rb, op=ALU.mult),
                 r=[src_b, b_ss8], w=[b_n1])
            S.op("dve", I("tensor_tensor", out=out_t, in0=n1[:, 0:n], in1=gain_ap, op=ALU.mult), r=[b_n1, gain_b], w=[out_b])

        def rope(src_t, src_b, dst_t, dst_b, sub):
            C = mkap(ropeC[:], sub * 32, [[pst(ropeC), 128], [0, 8], [1, 32]])
            S.op("dve", I("tensor_tensor", out=n1[:, 0:256].rearrange("p (g d) -> p g d", d=32), in0=src_t.rearrange("p (g d) -> p g d", d=32), in1=C, op=ALU.mult),
                 r=[src_b, b_rope], w=[b_n1])
            for hf in range(2):
                in0 = mkap(src_t, (1 - hf) * 8, [[src_t.ap[0][0], 128], [32, 8], [16, 2], [1, 8]])
                in1 = mkap(ropeS[:], sub * 32 + hf * 8, [[pst(ropeS), 128], [0, 8], [16, 2], [1, 8]])
                o = mkap(sq2[:], hf * 8, [[pst(sq2), 128], [32, 8], [16, 2], [1, 8]])
                S.op("dve", I("tensor_tensor", out=o, in0=in0, in1=in1, op=ALU.mult), r=[src_b, b_rope], w=[b_sq2])
            S.op("dve", I("tensor_tensor", out=dst_t, in0=n1[:, 0:256], in1=sq2[:, 0:256], op=ALU.add), r=[b_n1, b_sq2], w=[dst_b])

        def transpose_store(src16, src_b, nblk, dsts, k):
            ps, pb = PS()
            for q in range(nblk):
                S.op("pe", I("matmul", ps[:, q * 128:(q + 1) * 128], lhsT=src16[:, q * 128:(q + 1) * 128], rhs=ident[:], start=True, stop=True),
                     r=[src_b, b_ident], w=[pb])
            S.op("act", I("activation", out=tb16[k][:, 0:nblk * 128], in_=ps[:, 0:nblk * 128], func=AF.Copy), r=[pb], w=[b_tb16[k]])
            for q in range(nblk):
                dap, dbuf = dsts[q]
                S.dma("sp", I("dma_start", out=dap, in_=tb16[k][:, q * 128:(q + 1) * 128]), r=[b_tb16[k]], w=[dbuf])

        for g in range(2):
            load_gs(l, g, 1, "norm_mix", 2 * g)
            load_mod(l, g, 0, 2 * g + 1)
        for h in range(4):
            S.dma("sp", I("dma_start", out=gqk[:, h * 64:(h + 1) * 64], in_=bcast_row(din["na_q_norm"][l], 64)), w=[b_gqk])
            S.dma("sp", I("dma_start", out=gqk[:, 256 + h * 64:256 + (h + 1) * 64], in_=bcast_row(din["na_k_norm"][l], 64)), w=[b_gqk])
        S.op("dve", I("tensor_scalar", out=gqk[:, 0:256], in0=gqk[:, 0:256], scalar1=64 ** -0.5, scalar2=None, op0=ALU.mult), r=[b_gqk], w=[b_gqk])
        for gi in range(8):
            S.dma("sp", I("dma_start", out=gdq[:, gi * 32:(gi + 1) * 32], in_=bcast_row(din["df_q_norm"][l], 32)), w=[b_gdq])
            S.dma("sp", I("dma_start", out=gdk[:, gi * 32:(gi + 1) * 32], in_=bcast_row(din["df_k_norm"][l], 32)), w=[b_gdk])
        S.op("dve", I("tensor_scalar", out=gdq[:], in0=gdq[:], scalar1=32 ** -0.5, scalar2=None, op0=ALU.mult), r=[b_gdq], w=[b_gdq])

        def p1a_A(j):
            g = 0 if j < 8 else 1
            xt, bx = xts[j % 2], bxts[j % 2]
            S.dma("sp", I("dma_start", out=xt[:], in_=x_src[j * 128:(j + 1) * 128, :]), r=[bx_src[j]], w=[bx])
            rms_mod(xt, bx, mpool[2 * g], mpool[2 * g + 1], mpool_b[2 * g], mpool_b[2 * g + 1], wks[j % 2])

        def p1a_B(j):
            transpose_to_hT(j, wks[j % 2], hT, b_hT)

        pipeline(NSUB, [p1a_A, p1a_B])

        if stage <= 0.5:
            return
        w2d = din["w_in"][l]
        wq = [load_w_cols(w2d, 0, 512)]
        for cg in range(2):
            W, bW = wq.pop(0)
            wq.append(load_w_cols(w2d, (cg + 1) * 512, 512))
            for t in range(NTILE):
                for oc in range(4):
                    ps, pb = PS()
                    for kc in range(8):
                        S.op("pe", I("matmul", ps[:], lhsT=W[:, kc, oc * 128:(oc + 1) * 128], rhs=hT[:, kc, t * 512:(t + 1) * 512], start=(kc == 0), stop=(kc == 7)),
                             r=[bW] + b_hT[t * 4:(t + 1) * 4], w=[pb])
                    k = oc % 2
                    if cg == 0:
                        S.op("act", I("activation", out=stg_f[k][:], in_=ps[:], func=AF.Copy), r=[pb], w=[b_stg_f[k]])
                        S.dma("sp", I("dma_start", out=scr["LX"][oc * 128:(oc + 1) * 128, t * 512:(t + 1) * 512], in_=stg_f[k][:]), r=[b_stg_f[k]], w=[bLX[t]])
                    else:
                        S.op("act", I("activation", out=stg_b[k][:], in_=ps[:], func=AF.Gelu_apprx_tanh), r=[pb], w=[b_stg_b[k]])
                        S.dma("sp", I("dma_start", out=scr["LG"][oc * 128:(oc + 1) * 128, t * 512:(t + 1) * 512], in_=stg_b[k][:]), r=[b_stg_b[k]], w=[bLG[t]])
        if stage <= 0.6:
            return
        W, bW = wq.pop(0)
        wq.append(load_w_cols(w2d, 1536, 512))
        ctxs = {}

        def cg2_A(j, W=W, bW=bW):
            rot()
            ps, pb = PS()
            for kc in range(8):
                S.op("pe", I("matmul", ps[:], lhsT=hT[:, kc, j * 128:(j + 1) * 128], rhs=W[:, kc, :], start=(kc == 0), stop=(kc == 7)), r=[bW, b_hT[j]], w=[pb])
            S.op("act", I("activation", out=qk[:], in_=ps[:], func=AF.Copy), r=[pb], w=[b_qk])
            headnorm(qk[:], b_qk, 8, 64, gqk[:], b_gqk, n2[:], b_n2)
            if j < 8:
                b_, t0 = j // 2, (j % 2) * 128
                o = dout["o_nk"]
                dst = bass.AP(o.tensor, o.offset + ((b_ * 2 + l) * 4 * 256 + t0) * 64, [[64, 128], [256 * 64, 4], [1, 64]])
                S.dma("sp", I("dma_start", out=dst, in_=n2[:, 256:512].rearrange("p (h d) -> p h d", d=64)), r=[b_n2])
            S.op("dve", I("tensor_copy", out=nb16[:], in_=n2[:]), r=[b_n2], w=[b_nb16])
            ctxs[j] = (nb16, b_nb16)

        def cg2_B(j):
            nb, bnb = ctxs.pop(j)
            dsts = [(scr["NQT"][q * 128:(q + 1) * 128, j * 128:(j + 1) * 128], bNQT[j]) for q in range(2)] + \
                   [(scr["NKT"][q * 128:(q + 1) * 128, j * 128:(j + 1) * 128], bNKT[j]) for q in range(2)]
            transpose_store(nb, bnb, 4, dsts, j % 2)

        pipeline(NSUB, [cg2_A, cg2_B], skew=2)
        if stage <= 0.7:
            return
        W, bW = wq.pop(0)
        wq.append(load_w_cols(w2d, 2048, 512))
        def cg3_A(j, W=W, bW=bW):
            k = j % 2
            rot()
            ps, pb = PS()
            for kc in range(8):
                S.op("pe", I("matmul", ps[:], lhsT=hT[:, kc, j * 128:(j + 1) * 128], rhs=W[:, kc, :], start=(kc == 0), stop=(kc == 7)), r=[bW, b_hT[j]], w=[pb])
            S.op("act", I("activation", out=vb16[k][:], in_=ps[:, 0:256], func=AF.Copy), r=[pb], w=[b_vb16[k]])
            S.dma("sp", I("dma_start", out=scr["NV"][j * 128:(j + 1) * 128, :], in_=vb16[k][:]), r=[b_vb16[k]], w=[bNV[j]])
            if j < 8:
                b_, t0 = j // 2, (j % 2) * 128
                S.op("act", I("activation", out=vf32[k][:], in_=ps[:, 0:256], func=AF.Copy), r=[pb], w=[b_vf32[k]])
                o = dout["o_nv"]
                dst = bass.AP(o.tensor, o.offset + ((b_ * 2 + l) * 4 * 256 + t0) * 64, [[64, 128], [256 * 64, 4], [1, 64]])
                S.dma("sp", I("dma_start", out=dst, in_=vf32[k][:].rearrange("p (h d) -> p h d", d=64)), r=[b_vf32[k]])
            S.op("act", I("activation", out=qk[:, 0:256], in_=ps[:, 256:512], func=AF.Copy), r=[pb], w=[b_qk])
            headnorm(qk[:, 0:256], b_qk, 8, 32, gdq[:], b_gdq, n2[:, 0:256], b_n2)
            if j >= 8:
                rope(n2[:, 0:256], b_n2, nb16[:, 0:256], b_nb16, j - 8)
            else:
                S.op("dve", I("tensor_copy", out=nb16[:, 0:256], in_=n2[:, 0:256]), r=[b_n2], w=[b_nb16])
            ctxs[j] = (nb16, b_nb16)

        def cg3_B(j):
            nb, bnb = ctxs.pop(j)
            dsts = [(scr["DQT"][q * 128:(q + 1) * 128, j * 128:(j + 1) * 128], bDQT[j]) for q in range(2)]
            transpose_store(nb, bnb, 2, dsts, j % 2)

        pipeline(NSUB, [cg3_A, cg3_B], skew=2)
        if stage <= 0.8:
            return
        W, bW = wq.pop(0)
        def cg4_A(j, W=W, bW=bW):
            k = j % 2
            rot()
            ps, pb = PS()
            for kc in range(8):
                S.op("pe", I("matmul", ps[:], lhsT=hT[:, kc, j * 128:(j + 1) * 128], rhs=W[:, kc, :], start=(kc == 0), stop=(kc == 7)), r=[bW, b_hT[j]], w=[pb])
            S.op("act", I("activation", out=vb16[k][:], in_=ps[:, 256:512], func=AF.Copy), r=[pb], w=[b_vb16[k]])
            S.dma("sp", I("dma_start", out=scr["DV"][j * 128:(j + 1) * 128, :], in_=vb16[k][:]), r=[b_vb16[k]], w=[bDV[j]])
            if j < 8:
                b_, t0 = j // 2, (j % 2) * 128
                S.op("act", I("activation", out=vf32[k][:], in_=ps[:, 256:512], func=AF.Copy), r=[pb], w=[b_vf32[k]])
                o = dout["o_dv"]
                dst = bass.AP(o.tensor, o.offset + ((b_ * 2 + l) * 4 * 256 + t0) * 64, [[64, 128], [256 * 64, 4], [1, 64]])
                S.dma("sp", I("dma_start", out=dst, in_=vf32[k][:].rearrange("p (h d) -> p h d", d=64)), r=[b_vf32[k]])
            S.op("act", I("activation", out=qk[:, 0:256], in_=ps[:, 0:256], func=AF.Copy), r=[pb], w=[b_qk])
            headnorm(qk[:, 0:256], b_qk, 8, 32, gdk[:], b_gdk, n2[:, 0:256], b_n2)
            if j >= 8:
                rope(n2[:, 0:256], b_n2, nb16[:, 0:256], b_nb16, j - 8)
            else:
                b_, t0 = j // 2, (j % 2) * 128
                o = dout["o_dk"]
                dst = bass.AP(o.tensor, o.offset + ((b_ * 2 + l) * 8 * 256 + t0) * 32, [[32, 128], [256 * 32, 8], [1, 32]])
                S.dma("sp", I("dma_start", out=dst, in_=n2[:, 0:256].rearrange("p (g d) -> p g d", d=32)), r=[b_n2])
                S.op("dve", I("tensor_copy", out=nb16[:, 0:256], in_=n2[:, 0:256]), r=[b_n2], w=[b_nb16])
            ctxs[j] = (nb16, b_nb16)

        def cg4_B(j):
            nb, bnb = ctxs.pop(j)
            dsts = [(scr["DKT"][q * 128:(q + 1) * 128, j * 128:(j + 1) * 128], bDKT[j]) for q in range(2)]
            transpose_store(nb, bnb, 2, dsts, j % 2)

        pipeline(NSUB, [cg4_A, cg4_B], skew=2)
        if stage <= 0.9:
            return
        for cch in range(4):
            jj = NSUB + cch
            k = cch % 2
            rot()
            col0 = TT + cch * 128
            src = din["cnk"]
            S.dma("sp", I("dma_start", out=qk[:, 0:256].rearrange("p (h d) -> p h d", d=64),
                          in_=bass.AP(src.tensor, src.offset + (l * 4 * 512 + cch * 128) * 64, [[64, 128], [512 * 64, 4], [1, 64]])), w=[b_qk])
            S.op("dve", I("tensor_copy", out=nb16[:, 0:256], in_=qk[:, 0:256]), r=[b_qk], w=[b_nb16])
            transpose_store(nb16, b_nb16, 2, [(scr["NKT"][q * 128:(q + 1) * 128, col0:col0 + 128], bNKT[jj]) for q in range(2)], k)
            src = din["cdk"]
            S.dma("sp", I("dma_start", out=qk[:, 256:512].rearrange("p (g d) -> p g d", d=32),
                          in_=bass.AP(src.tensor, src.offset + (l * 8 * 512 + cch * 128) * 32, [[32, 128], [512 * 32, 8], [1, 32]])), w=[b_qk])
            S.op("dve", I("tensor_copy", out=nb16[:, 256:512], in_=qk[:, 256:512]), r=[b_qk], w=[b_nb16])
            transpose_store(nb16[:, 256:512], b_nb16, 2, [(scr["DKT"][q * 128:(q + 1) * 128, col0:col0 + 128], bDKT[jj]) for q in range(2)], 1 - k)
            for nm, dst, bb in (("cnv", "NV", bNV), ("cdv", "DV", bDV)):
                src = din[nm]
                S.dma("sp", I("dma_start", out=n2[:, 0:256].rearrange("p (h d) -> p h d", d=64),
                              in_=bass.AP(src.tensor, src.offset + (l * 4 * 512 + cch * 128) * 64, [[64, 128], [512 * 64, 4], [1, 64]])), w=[b_n2])
                S.op("dve", I("tensor_copy", out=vb16[k][:], in_=n2[:, 0:256]), r=[b_n2], w=[b_vb16[k]])
                S.dma("sp", I("dma_start", out=scr[dst][col0:col0 + 128, :], in_=vb16[k][:]), r=[b_vb16[k]], w=[bb[jj]])

    def lru_phase(l):
        P_ = LP[l]
        prm, b_prm, h0t, b_h0, c8, b_c8, wbd, b_wbd = P_["prm"], P_["b_prm"], P_["h0t"], P_["b_h0"], P_["c8"], P_["b_c8"], P_["wbd"], P_["b_wbd"]
        c8b = P_["c8b"]
        TM = NS_T
        lxb = sb([128, TM + 16], F32, "lxb"); b_lxb = Buf()
        xc = sb([128, TM], F32, "xc"); b_xc = Buf()
        xcb = sb([128, TM], BF16, "xcb"); b_xcb = Buf()
        R = [sb([128, TM], F32, "R") for _ in range(2)]; b_R = [Buf(), Buf()]
        Ib = [sb([128, TM], F32, "Ib") for _ in range(2)]; b_Ib = [Buf(), Buf()]
        T2 = [sb([128, TM], F32, "T2") for _ in range(2)]; b_T2 = [Buf(), Buf()]
        HH = T2; b_HH = b_T2
        G = sb([128, TM], BF16, "G"); b_G = Buf()
        Y = sb([128, TM], BF16, "Y"); b_Y = Buf()
        LGRP = ((0, 4, 256), (NP_T, 1, NS_T))
        lx_loaded = set()

        def lx_load(g_, c_):
            t0_, nseq_, T_ = LGRP[g_]
            N_ = nseq_ * T_
            TP_ = T_ + 4
            tiles_ = list(range(t0_ // 512, (t0_ + N_ + 511) // 512))
            v3 = lxb[:, 0:nseq_ * TP_].rearrange("p (s t) -> p s t", t=TP_)
            S.op("pool", I("memset", v3[:, :, 0:1], 0.0), w=[b_lxb])
            S.op("pool", I("memset", v3[:, :, T_ + 1:T_ + 4], 0.0), w=[b_lxb])
            S.dma("sp", I("dma_start", out=v3[:, :, 1:T_ + 1], in_=scr["LX"][c_ * 128:(c_ + 1) * 128, t0_:t0_ + N_].rearrange("p (s t) -> p s t", t=T_)),
                  r=[bLX[t] for t in tiles_], w=[b_lxb])
            lx_loaded.add((g_, c_))

        for gi_, (t0, nseq, T) in enumerate(LGRP):
            N = nseq * T
            TP = T + 4
            tiles = list(range(t0 // 512, (t0 + N + 511) // 512))
            for c in range(4):
                lx3 = lxb[:, 0:nseq * TP].rearrange("p (s t) -> p s t", t=TP)
                xc3 = xc[:, 0:N].rearrange("p (s t) -> p s t", t=T)
                if not (gi_, c) in lx_loaded:
                    lx_load(gi_, c)
                S.dma("sp", I("dma_start", out=G[:, 0:N], in_=scr["LG"][c * 128:(c + 1) * 128, t0:t0 + N]), r=[bLG[t] for t in tiles], w=[b_G])
                S.op("dve", I("tensor_scalar", out=xc3, in0=lx3[:, :, 0:T], scalar1=prm[:, c, 0:1], scalar2=prm[:, c, 4:5], op0=ALU.mult, op1=ALU.add),
                     r=[b_lxb, b_prm], w=[b_xc])
                for tap in range(1, 4):
                    S.op("dve", I("scalar_tensor_tensor", out=xc3, in0=lx3[:, :, tap:tap + T], scalar=prm[:, c, tap:tap + 1], in1=xc3, op0=ALU.mult, op1=ALU.add),
                         r=[b_lxb, b_prm, b_xc], w=[b_xc])
                S.op("act", I("activation", out=xcb[:, 0:N], in_=xc[:, 0:N], func=AF.Copy), r=[b_xc], w=[b_xcb])
                nxt = (gi_, c + 1) if c < 3 else ((gi_ + 1, 0) if gi_ == 0 else None)
                if nxt is not None:
                    lx_load(*nxt)

                def st_gates(dr, c=c, N=N):
                    for tl in range(N // 512):
                        cs = slice(tl * 512, (tl + 1) * 512)
                        for gate, (dst, bd, boff) in enumerate(((R[dr], b_R[dr], 5), (Ib[dr], b_Ib[dr], 7))):
                            gi = (c * 2 + dr) * 2 + gate
                            ps, pb = PS()
                            S.op("pe", I("matmul", ps[:], lhsT=wbd[:, gi, :], rhs=xcb[:, cs], start=True, stop=True), r=[b_wbd, b_xcb], w=[pb])
                            S.op("act", I("activation", out=dst[:, cs], in_=ps[:], func=AF.Sigmoid, bias=prm[:, c, boff + dr:boff + dr + 1]), r=[pb, b_prm], w=[bd])

                def st_exp(dr, c=c, N=N):
                    S.op("act", I("activation", out=R[dr][:, 0:N], in_=R[dr][:, 0:N], func=AF.Exp, scale=c8[:, c, dr:dr + 1]), r=[b_R[dr], b_c8], w=[b_R[dr]])

                def st_sq(dr, c=c, N=N):
                    S.op("act", I("activation", out=T2[dr][:, 0:N], in_=R[dr][:, 0:N], func=AF.Exp, scale=c8b[:, c, dr:dr + 1]), r=[b_R[dr], b_c8], w=[b_T2[dr]])

                def st_sqrt(dr, N=N):
                    S.op("act", I("activation", out=T2[dr][:, 0:N], in_=T2[dr][:, 0:N], func=AF.Sqrt, scale=-1.0, bias=1.0), r=[b_T2[dr]], w=[b_T2[dr]])

                def st_b1(dr, N=N):
                    S.op("dve", I("tensor_tensor", out=Ib[dr][:, 0:N], in0=Ib[dr][:, 0:N], in1=xc[:, 0:N], op=ALU.mult), r=[b_Ib[dr], b_xc], w=[b_Ib[dr]])

                def st_b2(dr, N=N):
                    S.op("dve", I("tensor_tensor", out=Ib[dr][:, 0:N], in0=Ib[dr][:, 0:N], in1=T2[dr][:, 0:N], op=ALU.mult), r=[b_Ib[dr], b_T2[dr]], w=[b_Ib[dr]])

                def st_scan(dr, c=c, nseq=nseq, T=T, gi_=gi_):
                    for sq_ in range(nseq):
                        o0 = sq_ * T
                        init = h0t[:, c, dr:dr + 1] if gi_ == 1 else 0.0
                        if dr == 0:
                            S.op("dve", I("tensor_tensor_scan", out=HH[0][:, o0:o0 + T], data0=R[0][:, o0:o0 + T], data1=Ib[0][:, o0:o0 + T], initial=init, op0=ALU.mult, op1=ALU.add),
                                 r=[b_R[0], b_Ib[0], b_h0], w=[b_HH[0]])
                        else:
                            def rv(t_):
                                return mkap(t_[:], o0 + T - 1, [[pst(t_), 128], [-1, T]])
                            S.op("dve", I("tensor_tensor_scan", out=rv(HH[1]), data0=rv(R[1]), data1=rv(Ib[1]), initial=init, op0=ALU.mult, op1=ALU.add),
                                 r=[b_R[1], b_Ib[1], b_h0], w=[b_HH[1]])
                        if gi_ == 0:
                            o = dout["o_lru"]
                            col = o0 + T - 1 if dr == 0 else o0
                            S.dma("sp", I("dma_start", out=bass.AP(o.tensor, o.offset + ((sq_ * 2 + l) * 2 + dr) * 512 + c * 128, [[1, 128], [1, 1]]), in_=HH[dr][:, col:col + 1]), r=[b_HH[dr]])

                for step in (st_gates, st_sq, st_exp, st_b1, st_sqrt, st_b2, st_scan):
                    for dr in range(2):
                        step(dr)
                S.op("dve", I("tensor_tensor", out=HH[0][:, 0:N], in0=HH[0][:, 0:N], in1=HH[1][:, 0:N], op=ALU.add), r=[b_HH[0], b_HH[1]], w=[b_HH[0]])
                S.op("dve", I("tensor_tensor", out=Y[:, 0:N], in0=HH[0][:, 0:N], in1=G[:, 0:N], op=ALU.mult), r=[b_HH[0], b_G], w=[b_Y])
                S.dma("sp", I("dma_start", out=scr["CATT"][c * 128:(c + 1) * 128, t0:t0 + N], in_=Y[:, 0:N]), r=[b_Y],
                      w=[bCATT[2 * c][t] for t in tiles] + [bCATT[2 * c + 1][t] for t in tiles])

    def attn_phase(l, lam_init):
        P_ = LP[l]
        nlam, b_nlam, gsub, b_gsub = P_["nlam"], P_["b_nlam"], P_["gsub"], P_["b_gsub"]
        bZR = bZRl[l]
        zr_l = scr["ZR"][l]
        osb = [sb([65, 512], F32, "osb") for _ in range(2)]; b_osb = [Buf(), Buf()]
        rd = [sb([64, 512], F32, "rd") for _ in range(2)]; b_rd = [Buf(), Buf()]
        on = [sb([64, 512], F32, "on") for _ in range(2)]; b_on = [Buf(), Buf()]
        yv = sb([64, 512], F32, "yv"); b_yv = Buf()
        ysq = sb([64, 512], F32, "ysq"); b_ysq = Buf()
        rr = sb([64, 512], F32, "rr"); b_rr = Buf()
        y16 = [sb([64, 512], BF16, "y16") for _ in range(2)]; b_y16 = [Buf(), Buf()]
        NEB = 6
        ebuf = [sb([128, 512], BF16, "ebuf") for _ in range(NEB)]; b_ebuf = [Buf() for _ in range(NEB)]
        ecnt = [0]
        ycnt = [0]

        def attn_block(QT_ap, nq, chunks, acc_bank, b_q, b_k, b_v, stages=None):
            po, pob = psum[acc_bank], psb[acc_bank]
            n = len(chunks)
            LA = 4 if any(ch[2] is not None for ch in chunks) else min(4, len(ps_banks[0]) - 1)
            slots = [None] * n
            stages = list(stages or [])

            def issue_qk(i):
                kt = chunks[i][0]
                ps, pb = PS()
                S.op("pe", I("matmul", ps[:, 0:nq], lhsT=kt, rhs=QT_ap, start=True, stop=True), r=[b_q, b_k], w=[pb])
                e = ecnt[0] % NEB
                ecnt[0] += 1
                S.op("act", I("activation", out=ebuf[e][:, 0:nq], in_=ps[:, 0:nq], func=AF.Exp), r=[pb], w=[b_ebuf[e]])
                mk, mkb = chunks[i][2], chunks[i][3]
                if mk is not None:
                    S.op("dve", I("tensor_tensor", out=ebuf[e][:, 0:nq], in0=ebuf[e][:, 0:nq], in1=mk, op=ALU.mult), r=[b_ebuf[e], mkb], w=[b_ebuf[e]])
                slots[i] = e

            for i in range(min(LA, n)):
                issue_qk(i)
            for i in range(n):
                if i + LA < n:
                    issue_qk(i + LA)
                e = slots[i]
                S.op("pe", I("matmul", po[0:65, 0:nq], lhsT=chunks[i][1], rhs=ebuf[e][:, 0:nq], start=(i == 0), stop=(i == n - 1)), r=[b_v, b_ebuf[e]], w=[pob])
                while stages and stages[0][0] * n <= i + 1:
                    stages.pop(0)[1]()
            while stages:
                stages.pop(0)[1]()

        def norm_stages(acc_bank, nq, k, recip_on_act=False):
            po, pob = psum[acc_bank], psb[acc_bank]

            def s0():
                S.op("act", I("activation", out=osb[k][:, 0:nq], in_=po[0:65, 0:nq], func=AF.Copy), r=[pob], w=[b_osb[k]])

            def s1():
                if recip_on_act:
                    S.op("act", I("activation", out=osb[k][64:65, 0:nq], in_=osb[k][64:65, 0:nq], func=AF.Ln), r=[b_osb[k]], w=[b_osb[k]])
                    S.op("act", I("activation", out=osb[k][64:65, 0:nq], in_=osb[k][64:65, 0:nq], func=AF.Exp, scale=-1.0), r=[b_osb[k]], w=[b_osb[k]])
                else:
                    S.op("dve", I("reciprocal", out=osb[k][64:65, 0:nq], in_=osb[k][64:65, 0:nq]), r=[b_osb[k]], w=[b_osb[k]])

            def s2():
                ps, pb = PS()
                S.op("pe", I("matmul", ps[0:64, 0:nq], lhsT=sel65[:], rhs=osb[k][:, 0:nq], start=True, stop=True), r=[b_sel65, b_osb[k]], w=[pb])
                S.op("dve", I("tensor_tensor", out=on[k][:, 0:nq], in0=osb[k][0:64, 0:nq], in1=ps[0:64, 0:nq], op=ALU.mult), r=[b_osb[k], pb], w=[b_on[k]])
            return s0, s1, s2

        def normalize(acc_bank, nq, k):
            po, pob = psum[acc_bank], psb[acc_bank]
            S.op("act", I("activation", out=osb[k][:, 0:nq], in_=po[0:65, 0:nq], func=AF.Copy), r=[pob], w=[b_osb[k]])
            S.op("dve", I("reciprocal", out=osb[k][64:65, 0:nq], in_=osb[k][64:65, 0:nq]), r=[b_osb[k]], w=[b_osb[k]])
            ps, pb = PS()
            S.op("pe", I("matmul", ps[0:64, 0:nq], lhsT=sel65[:], rhs=osb[k][:, 0:nq], start=True, stop=True), r=[b_sel65, b_osb[k]], w=[pb])
            S.op("dve", I("tensor_tensor", out=on[k][:, 0:nq], in0=osb[k][0:64, 0:nq], in1=ps[0:64, 0:nq], op=ALU.mult), r=[b_osb[k], pb], w=[b_on[k]])

        def store_cat(src16_ap, src_b, row0, col0, nq):
            t = col0 // 512
            S.dma("sp", I("dma_start", out=scr["CATT"][row0:row0 + 64, col0:col0 + nq], in_=src16_ap), r=[src_b], w=[bCATT[row0 // 64][t]])

        def na_chunks_for(Q):
            lo, hi = max(0, 4 * Q - 2), min(31, 4 * Q + 5)
            return list(range(lo, hi + 1))

        def tile_class(Q):
            return 0 if Q == 0 else (2 if Q == 7 else 1)

        def valid_b_range(kr, Q):
            bs = [b for b in range(8) if min(max(8 * Q + b - 4, 0), 56) <= kr < min(max(8 * Q + b - 4, 0), 56) + 8]
            return (bs[0], bs[-1]) if bs else None

        def mask_tile_idx(Q, m):
            c = tile_class(Q)
            if c == 0:
                return m
            if c == 1:
                return 6 + (m - (4 * Q - 2))
            return 14 + (m - 26)

        with phase():
            nsets = []
            for _i in range(2):
                d_ = dict(QT=sb([128, TT], BF16, "QT"), KT=sb([128, TT + L_CTX], BF16, "KT"), VA=sb([128, 44, 65], BF16, "VA"),
                          MT=sb([128, 20, 512], BF16, "MT"), b_QT=Buf(), b_KT=Buf(), b_VA=Buf(), b_MT=MB(), b_MTz=Buf())
                S.op("pool", I("memset", d_["QT"][:], 0.0), w=[d_["b_QT"]])
                S.op("pool", I("memset", d_["KT"][:], 0.0), w=[d_["b_KT"]])
                S.op("pool", I("memset", d_["VA"][:, :, 64:65], 1.0), w=[d_["b_VA"]])
                nsets.append(d_)

            def na_load(h):
                d_ = nsets[h % 2]
                QT, KT, VA, MT = d_["QT"], d_["KT"], d_["VA"], d_["MT"]
                S.dma("sp", I("dma_start", out=QT[0:64, :], in_=scr["NQT"][h * 64:(h + 1) * 64, :]), r=bNQT, w=[d_["b_QT"]])
                S.dma("sp", I("dma_start", out=KT[0:64, :], in_=scr["NKT"][h * 64:(h + 1) * 64, :]), r=bNKT, w=[d_["b_KT"]])
                S.dma("sp", I("dma_start", out=VA[:, :, 0:64], in_=scr["NV"][:, h * 64:(h + 1) * 64].rearrange("(c p) d -> p c d", p=128)), r=bNV, w=[d_["b_VA"]])
                S.op("pool", I("memset", MT[:], 0.0), w=[d_["b_MT"], d_["b_MTz"]])
                zr = zr_l
                for Q in (0, 1, 7):
                    for m in na_chunks_for(Q):
                        ti = mask_tile_idx(Q, m)
                        for a_ in range(2):
                            kr = 2 * m + a_
                            vr = valid_b_range(kr, Q)
                            if vr is None:
                                continue
                            b0, b1 = vr
                            nbv = b1 - b0 + 1
                            j0 = 7 - kr + 8 * Q + b0
                            srcap = bass.AP(zr.tensor, zr.offset + (h * 15 + j0) * 4096, [[64, 64], [4096, nbv], [1, 64]])
                            dstap = MT[a_ * 64:(a_ + 1) * 64, ti, b0 * 64:(b1 + 1) * 64].rearrange("p (b q) -> p b q", q=64)
                            S.dma("sp", I("dma_start", out=dstap, in_=srcap), r=[bZR, d_["b_MTz"]], w=[d_["b_MT"]])

            na_pend = []
            nacnt = [0]
            ps_banks[0] = list(range(6))
            na_load(0)
            for h in range(4):
                if h + 1 < 4:
                    na_load(h + 1)
                d_ = nsets[h % 2]
                QT, KT, VA, MT = d_["QT"], d_["KT"], d_["VA"], d_["MT"]
                b_QT, b_KT, b_VA, b_MT = d_["b_QT"], d_["b_KT"], d_["b_VA"], d_["b_MT"]
                blocks = []
                for s_ in range(4):
                    chunks = [(KT[:, (2 * s_ + i) * 128:(2 * s_ + i + 1) * 128], VA[:, 2 * s_ + i, :], None, None) for i in range(2)]
                    blocks.append((QT[:, s_ * 256:(s_ + 1) * 256], 256, chunks, s_ * 256))
                for Q in range(8):
                    chunks = []
                    for m in na_chunks_for(Q):
                        ti = mask_tile_idx(Q, m)
                        chunks.append((KT[:, NP_T + m * 128:NP_T + (m + 1) * 128], VA[:, 8 + m, :], MT[:, ti, :], b_MT))
                    for i in range(4):
                        chunks.append((KT[:, TT + i * 128:TT + (i + 1) * 128], VA[:, 40 + i, :], None, None))
                    blocks.append((QT[:, NP_T + Q * 512:NP_T + (Q + 1) * 512], 512, chunks, NP_T + Q * 512))
                for (qap, nq, chunks, q0) in blocks:
                    bank = 6 + (nacnt[0] % 2)
                    nacnt[0] += 1
                    attn_block(qap, nq, chunks, bank, b_QT, b_KT, b_VA, stages=(na_pend.pop(0) if na_pend else None))

                    def mk(bank=bank, nq=nq, q0=q0, h=h):
                        s0, s1, s2 = norm_stages(bank, nq, 0, recip_on_act=True)

                        def s3():
                            k = ycnt[0] % 2; ycnt[0] += 1
                            S.op("pool", I("tensor_copy", out=y16[k][:, 0:nq], in_=on[0][:, 0:nq]), r=[b_on[0]], w=[b_y16[k]])
                            store_cat(y16[k][:, 0:nq], b_y16[k], 512 + h * 64, q0, nq)
                        return [(0.1, s0), (0.15, s1), (0.8, s2), (0.95, s3)]
                    na_pend.append(mk())
            while na_pend:
                for (_f, fn) in na_pend.pop(0):
                    fn()
        if stage <= 3:
            return
        with phase():
            dsets = []
            for _i in range(2):
                d_ = dict(DQ=sb([128, 2, TT], BF16, "DQ"), DK=sb([128, 2, TT + L_CTX], BF16, "DK"), VA=sb([128, 44, 65], BF16, "VAd"),
                          b_DQ=Buf(), b_DK=Buf(), b_VA=Buf())
                S.op("pool", I("memset", d_["DQ"][:], 0.0), w=[d_["b_DQ"]])
                S.op("pool", I("memset", d_["DK"][:], 0.0), w=[d_["b_DK"]])
                S.op("pool", I("memset", d_["VA"][:, :, 64:65], 1.0), w=[d_["b_VA"]])
                dsets.append(d_)

            def df_load(h):
                d_ = dsets[h % 2]
                S.dma("sp", I("dma_start", out=d_["DQ"][0:32, :, :], in_=scr["DQT"][h * 64:(h + 1) * 64, :].rearrange("(c d) t -> d c t", d=32)), r=bDQT, w=[d_["b_DQ"]])
                S.dma("sp", I("dma_start", out=d_["DK"][0:32, :, :], in_=scr["DKT"][h * 64:(h + 1) * 64, :].rearrange("(c d) t -> d c t", d=32)), r=bDKT, w=[d_["b_DK"]])
                S.dma("sp", I("dma_start", out=d_["VA"][:, :, 0:64], in_=scr["DV"][:, h * 64:(h + 1) * 64].rearrange("(c p) d -> p c d", p=128)), r=bDV, w=[d_["b_VA"]])

            df_pend = []
            dfcnt = [0]
            dfacc = [0]
            ps_banks[0] = list(range(5))
            df_load(0)
            for h in range(4):
                if h + 1 < 4:
                    df_load(h + 1)
                d_ = dsets[h % 2]
                DQ, DK, VA = d_["DQ"], d_["DK"], d_["VA"]
                b_DQ, b_DK, b_VA = d_["b_DQ"], d_["b_DK"], d_["b_VA"]
                blocks = [(s * 256, 256, [2 * s, 2 * s + 1]) for s in range(4)] + \
                         [(NP_T + Q * 512, 512, list(range(8, 44))) for Q in range(8)]
                for (q0, nq, chs) in blocks:
                    par = dfcnt[0] % 2
                    dfcnt[0] += 1
                    pend = df_pend.pop(0) if df_pend else [None, None]
                    banks = []
                    for c in range(2):
                        chunks = [(DK[:, c, ch * 128:(ch + 1) * 128], VA[:, ch, :], None, None) for ch in chs]
                        bank = 5 + (dfacc[0] % 3)
                        dfacc[0] += 1
                        banks.append(bank)
                        attn_block(DQ[:, c, q0:q0 + nq], nq, chunks, bank, b_DQ, b_DK, b_VA, stages=pend[c])

                    def mk(banks=banks, nq=nq, q0=q0, h=h):
                        a0, a1, a2 = norm_stages(banks[0], nq, 0)
                        b0, b1, b2 = norm_stages(banks[1], nq, 1)

                        def t0():
                            S.op("dve", I("scalar_tensor_tensor", out=yv[:, 0:nq], in0=on[1][:, 0:nq], scalar=nlam[0:64, 0:1], in1=on[0][:, 0:nq], op0=ALU.mult, op1=ALU.add),
                                 r=[b_on[0], b_on[1], b_nlam], w=[b_yv])
                            S.op("pool", I("tensor_tensor", out=ysq[:, 0:nq], in0=yv[:, 0:nq], in1=yv[:, 0:nq], op=ALU.mult), r=[b_yv], w=[b_ysq])

                        def t1():
                            ps, pb = PS()
                            S.op("pe", I("matmul", ps[0:64, 0:nq], lhsT=ones64[:], rhs=ysq[:, 0:nq], start=True, stop=True), r=[b_ones64, b_ysq], w=[pb])
                            S.op("act", I("activation", out=rr[:, 0:nq], in_=ps[0:64, 0:nq], func=AF.Ln, scale=1.0 / 64, bias=eps_t[0:64, 0:1]), r=[pb, b_eps], w=[b_rr])
                            S.op("act", I("activation", out=rr[:, 0:nq], in_=rr[:, 0:nq], func=AF.Exp, scale=-0.5), r=[b_rr], w=[b_rr])

                        def t2():
                            k = ycnt[0] % 2; ycnt[0] += 1
                            S.op("dve", I("scalar_tensor_tensor", out=y16[k][:, 0:nq], in0=yv[:, 0:nq], scalar=gsub[:, 0:1], in1=rr[:, 0:nq], op0=ALU.mult, op1=ALU.mult),
                                 r=[b_yv, b_gsub, b_rr], w=[b_y16[k]])
                            store_cat(y16[k][:, 0:nq], b_y16[k], 768 + h * 64, q0, nq)
                        first = [(0.03, a0), (0.05, b0), (0.08, a1), (0.1, b1), (0.5, a2), (0.6, b2), (0.75, t0)]
                        second = [(0.1, t1), (0.5, t2)]
                        return [first, second]
                    df_pend.append(mk())
            while df_pend:
                for lst in df_pend.pop(0):
                    for (_f, fn) in lst:
                        fn()

    def ffn_phase(l, x_src, bx_src, y_dst, by_dst):
        wstack = contextlib.ExitStack()
        hstack = contextlib.ExitStack()
        prev_stack = cur[0]
        cur[0] = hstack
        hT = sb([128, 8, TT], BF16, "h2T")
        cur[0] = prev_stack
        b_hT = [Buf() for _ in range(NSUB)]
        with phase():
            for g in range(2):
                load_mod(l, g, 2, g)
                load_gs(l, g, 4, "norm_ffn", 2 + g)
                load_mod(l, g, 3, 4 + g)
            wout = sb([128, 8, D], BF16, "wout"); b_wout = MB()
            for hf in range(2):
                S.dma("pool", I("dma_start", out=wout[:, :, hf * 512:(hf + 1) * 512], in_=din["w_out"][l][:, hf * 512:(hf + 1) * 512].rearrange("(k p) n -> p k n", p=128)), w=[b_wout])
            wks = [mk_wk(), mk_wk(), mk_wk()]
            xts = [sb([128, D], F32, "xt5") for _ in range(2)]; bxts = [Buf(), Buf()]
            xms = [sb([128, D], F32, "xm") for _ in range(3)]; bxms = [Buf(), Buf(), Buf()]
            cts = [sb([128, 8, 128], BF16, "ct") for _ in range(2)]; bcts = [Buf(), Buf()]
            tmps = [sb([128, 512], F32, "tmp5") for _ in range(2)]; b_tmps = [Buf(), Buf()]

            def p5a_load(j):
                k = j % 2
                S.dma("sp", I("dma_start", out=xts[k][:], in_=x_src[j * 128:(j + 1) * 128, :]), r=[bx_src[j]], w=[bxts[k]])
                S.dma("sp", I("dma_start", out=cts[k][:], in_=scr["CATT"][:, j * 128:(j + 1) * 128].rearrange("(k p) t -> p k t", p=128)),
                      r=[bCATT[c][j // 4] for c in range(16)], w=[bcts[k]])

            p5a_load(0)

            def p5a_A(j):
                g = 0 if j < 8 else 1
                k = j % 2
                if j + 1 < NSUB:
                    p5a_load(j + 1)
                for hf in range(2):
                    ps, pb = PS()
                    for kc in range(8):
                        S.op("pe", I("matmul", ps[:], lhsT=cts[k][:, kc, :], rhs=wout[:, kc, hf * 512:(hf + 1) * 512], start=(kc == 0), stop=(kc == 7)),
                             r=[bcts[k], b_wout], w=[pb])
                    S.op("dve", I("tensor_tensor", out=tmps[hf][:], in0=ps[:], in1=mpool[g][:, hf * 512:(hf + 1) * 512], op=ALU.mult), r=[pb, mpool_b[g]], w=[b_tmps[hf]])
                    S.op("pool", I("tensor_tensor", out=xms[j % 3][:, hf * 512:(hf + 1) * 512], in0=tmps[hf][:], in1=xts[k][:, hf * 512:(hf + 1) * 512], op=ALU.add),
                         r=[b_tmps[hf], bxts[k]], w=[bxms[j % 3]])
                S.dma("pool", I("dma_start", out=scr["XMID"][j * 128:(j + 1) * 128, :], in_=xms[j % 3][:]), r=[bxms[j % 3]], w=[bXMID[j]])

            def p5a_A2(j):
                g = 0 if j < 8 else 1
                k = j % 3
                rms_mod(xms[k], bxms[k], mpool[2 + g], mpool[4 + g], mpool_b[2 + g], mpool_b[4 + g], wks[j % 3])

            def p5a_B(j):
                transpose_to_hT(j, wks[j % 3], hT, b_hT)

            pipeline(NSUB, [p5a_A, p5a_A2, p5a_B], skew=1)
        if stage <= 4:
            hstack.close()
            return
        cur[0] = wstack
        wd = sb([128, 22, D], BF16, "wd"); b_wd = MB()
        cur[0] = prev_stack
        with phase():
            for q in range(2):
                S.dma("pool", I("dma_start", out=wd[:, q * 11:(q + 1) * 11, :], in_=din["ffn_down"][l][q * 11 * 128:(q + 1) * 11 * 128, :].rearrange("(c p) n -> p c n", p=128)), w=[b_wd])
            wp = [sb([128, 8, 256], BF16, "wp") for _ in range(2)]; b_wp = [MB(), MB()]
            cw = [sb([128, 2, 4], F32, "cw") for _ in range(2)]; b_cw = [MB(), MB()]
            NCV = 3
            cv = [[sb([128, 256], F32, "cv") for _ in range(2)] for _ in range(NCV)]; b_cv = [[Buf(), Buf()] for _ in range(NCV)]
            gg = [sb([128, 256], F32, "gg") for _ in range(2)]; b_gg = [Buf(), Buf()]
            a16 = [sb([128, 256], BF16, "a16") for _ in range(NCV)]; b_a16 = [Buf() for _ in range(NCV)]
            fcnt = 0

            def p5b_load(v):
                k = v % 2
                for part in range(2):
                    c0 = (part * 22 + v) * 128
                    S.dma("pool", I("dma_start", out=wp[k][:, :, part * 128:(part + 1) * 128], in_=din["ffn_up"][l][:, c0:c0 + 128].rearrange("(k p) n -> p k n", p=128)), w=[b_wp[k]])
                    src = din["ffn_conv_w"]
                    S.dma("sp", I("dma_start", out=cw[k][:, part, 0:3], in_=bass.AP(src.tensor, src.offset + l * 3 * 2 * D_FF + c0, [[1, 128], [2 * D_FF, 3]])), w=[b_cw[k]])
                    src = din["ffn_conv_b"]
                    S.dma("sp", I("dma_start", out=cw[k][:, part, 3:4], in_=bass.AP(src.tensor, src.offset + l * 2 * D_FF + c0, [[1, 128], [1, 1]])), w=[b_cw[k]])

            pend = []

            def p5b_B(f, fg, v, tok0):
                S.op("act", I("activation", out=gg[fg][:], in_=cv[f][1][:], func=AF.Gelu_apprx_tanh), r=[b_cv[f][1]], w=[b_gg[fg]])
                S.op("pool", I("tensor_tensor", out=a16[f][:], in0=gg[fg][:], in1=cv[f][0][:], op=ALU.mult), r=[b_gg[fg], b_cv[f][0]], w=[b_a16[f]])
                S.dma("sp", I("dma_start", out=scr["ACTT"][v * 128:(v + 1) * 128, tok0:tok0 + 256], in_=a16[f][:]), r=[b_a16[f]], w=[bACTT[v][tok0 // 512]])

            p5b_load(0)
            for v in range(22):
                k = v % 2
                if v + 1 < 22:
                    p5b_load(v + 1)
                for (t0, T) in SEQS:
                    nft = T // 256
                    for ft in range(nft):
                        tok0 = t0 + ft * 256
                        lo = 1 if ft > 0 else 0
                        hi = 1 if ft < nft - 1 else 0
                        c0 = tok0 - lo
                        ncol = 256 + lo + hi
                        f = fcnt % NCV
                        fg = fcnt % 2
                        fcnt += 1
                        subs = sorted(set([c0 // 128, (c0 + ncol - 1) // 128, tok0 // 128, tok0 // 128 + 1]))
                        for part in range(2):
                            ps, pb = PS()
                            for kc in range(8):
                                S.op("pe", I("matmul", ps[:, 0:ncol], lhsT=wp[k][:, kc, part * 128:(part + 1) * 128], rhs=hT[:, kc, c0:c0 + ncol], start=(kc == 0), stop=(kc == 7)),
                                     r=[b_wp[k]] + [b_hT[s_] for s_ in subs], w=[pb])
                            c_ = cv[f][part]; bc = b_cv[f][part]
                            S.op("act", I("activation", out=c_[:], in_=ps[:, lo:lo + 256], func=AF.Identity, scale=cw[k][:, part, 1:2], bias=cw[k][:, part, 3:4]),
                                 r=[pb, b_cw[k]], w=[bc])
                            if lo:
                                S.op("dve", I("scalar_tensor_tensor", out=c_[:], in0=ps[:, 0:256], scalar=cw[k][:, part, 0:1], in1=c_[:], op0=ALU.mult, op1=ALU.add),
                                     r=[pb, b_cw[k], bc], w=[bc])
                            else:
                                S.op("dve", I("scalar_tensor_tensor", out=c_[:, 1:256], in0=ps[:, 0:255], scalar=cw[k][:, part, 0:1], in1=c_[:, 1:256], op0=ALU.mult, op1=ALU.add),
                                     r=[pb, b_cw[k], bc], w=[bc])
                            if hi:
                                S.op("dve", I("scalar_tensor_tensor", out=c_[:], in0=ps[:, lo + 1:lo + 257], scalar=cw[k][:, part, 2:3], in1=c_[:], op0=ALU.mult, op1=ALU.add),
                                     r=[pb, b_cw[k], bc], w=[bc])
                            else:
                                S.op("dve", I("scalar_tensor_tensor", out=c_[:, 0:255], in0=ps[:, lo + 1:lo + 256], scalar=cw[k][:, part, 2:3], in1=c_[:, 0:255], op0=ALU.mult, op1=ALU.add),
                                     r=[pb, b_cw[k], bc], w=[bc])
                        pend.append((f, fg, v, tok0))
                        if len(pend) > 1:
                            p5b_B(*pend.pop(0))
            while pend:
                p5b_B(*pend.pop(0))
        if stage <= 5:
            wstack.close()
            hstack.close()
            return
        with phase():
            for g in range(2):
                load_mod(l, g, 5, g)
            at = [sb([128, 22, 256], BF16, "at") for _ in range(2)]; b_at = [Buf(), Buf()]
            xms = [sb([128, D], F32, "xm6") for _ in range(3)]; bxms = [Buf() for _ in range(3)]
            xo = [sb([128, D], F32, "xo") for _ in range(2)]; bxo = [Buf(), Buf()]
            tmps = [sb([128, 512], F32, "tmp6") for _ in range(2)]; b_tmps = [Buf(), Buf()]
            NT2 = TT // 256

            def p5c_load_at(t):
                S.dma("sp", I("dma_start", out=at[t % 2][:], in_=scr["ACTT"][:, t * 256:(t + 1) * 256].rearrange("(c p) t -> p c t", p=128)),
                      r=[bACTT[v][t // 2] for v in range(22)], w=[b_at[t % 2]])

            def p5c_load_x(j):
                S.dma("sp", I("dma_start", out=xms[j % 3][:], in_=scr["XMID"][j * 128:(j + 1) * 128, :]), r=[bXMID[j]], w=[bxms[j % 3]])

            p5c_load_at(0)
            p5c_load_x(0)
            p5c_load_x(1)
            for t in range(NT2):
                ka = t % 2
                if t + 1 < NT2:
                    p5c_load_at(t + 1)
                for s_ in range(2):
                    j = t * 2 + s_
                    g = 0 if j < 8 else 1
                    k = j % 2
                    if j + 2 < NSUB:
                        p5c_load_x(j + 2)
                    for hf in range(2):
                        ps, pb = PS()
                        for c in range(22):
                            S.op("pe", I("matmul", ps[:], lhsT=at[ka][:, c, s_ * 128:(s_ + 1) * 128], rhs=wd[:, c, hf * 512:(hf + 1) * 512], start=(c == 0), stop=(c == 21)),
                                 r=[b_at[ka], b_wd], w=[pb])
                        S.op("dve", I("tensor_tensor", out=tmps[hf][:], in0=ps[:], in1=mpool[g][:, hf * 512:(hf + 1) * 512], op=ALU.mult), r=[pb, mpool_b[g]], w=[b_tmps[hf]])
                        S.op("pool", I("tensor_tensor", out=xo[k][:, hf * 512:(hf + 1) * 512], in0=tmps[hf][:], in1=xms[j % 3][:, hf * 512:(hf + 1) * 512], op=ALU.add),
                             r=[b_tmps[hf], bxms[j % 3]], w=[bxo[k]])
                    S.dma("act", I("dma_start", out=y_dst[j * 128:(j + 1) * 128, :], in_=xo[k][:]), r=[bxo[k]], w=[by_dst[j]])
        wstack.close()
        hstack.close()

    for l in range(2):
        x_src = din["xin"] if l == 0 else scr["XRES"]
        bx_src = [Buf() for _ in range(NSUB)] if l == 0 else bXRES
        y_dst = scr["XRES"] if l == 0 else dout["y"]
        by_dst = bXRES if l == 0 else [Buf() for _ in range(NSUB)]
        lam_init = 0.8 - 0.6 * float(np.exp(-0.3 * l))
        with phase():
            p1_phase(l, x_src, bx_src)
        if stage <= 1:
            return
        with phase():
            lru_phase(l)
        if stage <= 2:
            return
        with phase():
            attn_phase(l, lam_init)
        if stage <= 3:
            return
        with phase():
            ffn_phase(l, x_src, bx_src, y_dst, by_dst)
        if stage <= 6:
            return


def _constants():
    ident = np.eye(128, dtype=np.float32)
    p60 = np.zeros((60, 60), np.float32)
    for h in range(4):
        for j in range(15):
            p60[h * 15 + (14 - j), h * 15 + j] = 1.0
    oh = np.zeros((32, 64, 64), np.float32)
    for qc in range(64):
        cstart = min(max(qc - 8, 0), 48)
        for kc in range(64):
            if cstart <= kc < cstart + 16:
                oh[kc - qc + 15, kc, qc] = 1.0
            else:
                oh[31, kc, qc] = -30000.0
    t = np.arange(NS_T)
    freqs = (10000.0 ** (-np.arange(8, dtype=np.float32) / 8)).astype(np.float32)
    ar = (t // 64).astype(np.float32)[:, None] * freqs[None, :]
    ac = (t % 64).astype(np.float32)[:, None] * freqs[None, :]
    cr, sr, cc, sc = np.cos(ar), np.sin(ar), np.cos(ac), np.sin(ac)
    ropec = np.concatenate([cr, cr, cc, cc], axis=1).astype(np.float32)
    ropes = np.concatenate([-sr, sr, -sc, sc], axis=1).astype(np.float32)
    return dict(c_ident=ident, c_p60=p60, c_oh=oh.reshape(32, 4096), c_ropec=ropec, c_ropes=ropes)


_WEIGHT_NAMES = ["ada_w", "ada_b", "norm_mix", "norm_ffn", "w_in", "lru_conv_w", "lru_conv_b", "lru_wa", "lru_ba", "lru_wx",
                 "lru_bx", "lru_lambda", "na_q_norm", "na_k_norm", "na_rpb", "df_q_norm", "df_k_norm", "df_lambda", "df_subln",
                 "w_out", "ffn_up", "ffn_conv_w", "ffn_conv_b", "ffn_down"]


def make_in_map(inputs, core, consts):
    f = lambda a: np.ascontiguousarray(np.asarray(a, dtype=np.float32))
    b = core // 2
    m = {}
    m["xin"] = f(np.concatenate([np.asarray(inputs["x_prompt"])[4 * core:4 * core + 4].reshape(NP_T, D), np.asarray(inputs["x_sample"])[b]], axis=0))
    m["cvec"] = f(np.stack([np.asarray(inputs["c_ctx"]), np.asarray(inputs["c"])[b]], axis=0))
    m["state"] = f(np.asarray(inputs["state_lru"])[b])
    m["cnk"] = f(np.asarray(inputs["cache_na_k"])[b])
    m["cnv"] = f(np.asarray(inputs["cache_na_v"])[b])
    m["cdk"] = f(np.asarray(inputs["cache_df_k"])[b])
    m["cdv"] = f(np.asarray(inputs["cache_df_v"])[b])
    for n in _WEIGHT_NAMES:
        m[n] = f(inputs[n])
    m.update(consts)
    return m


_NC_CACHE = {}


def kernel(**inputs):
    n = 8
    if "nc" not in _NC_CACHE:
        _NC_CACHE["nc"] = build()
    nc = _NC_CACHE["nc"]
    consts = _constants()
    in_maps = [make_in_map(inputs, c, consts) for c in range(n)]
    res = run_bass_kernel_spmd(nc, in_maps, core_ids=list(range(n))).results
    y_p = np.concatenate([r["y"][:NP_T].reshape(4, 256, D) for r in res], axis=0)
    y_s = np.stack([res[2 * b]["y"][NP_T:] for b in range(4)], axis=0)
    cat = lambda k: np.concatenate([r[k] for r in res], axis=0)
    return (y_p, y_s, cat("o_lru"), cat("o_nk"), cat("o_nv"), cat("o_dk"), cat("o_dv"))
```

```python
import contextlib
import numpy as np
import concourse.bass as bass
import concourse.mybir as mybir
from concourse.bass_utils import run_bass_kernel_spmd

F32 = mybir.dt.float32
BF16 = mybir.dt.bfloat16
AF = mybir.ActivationFunctionType
ALU = mybir.AluOpType
AX = mybir.AxisListType

ENGS = ("sp", "act", "pe", "dve", "pool")

D = 1024
NP_T = 1024
NS_T = 4096
TT = NP_T + NS_T
NTILE = TT // 512
NSUB = TT // 128
L_CTX = 512
D_IN = 2560
D_FF = 2816
EPS = 1e-6
SEQS = [(0, 256), (256, 256), (512, 256), (768, 256), (1024, 4096)]


class Buf:
    __slots__ = ("name", "lws", "rd", "rd_dma", "multi")

    def __init__(self, name="", multi=False):
        self.name = name
        self.lws = []
        self.rd = {}
        self.rd_dma = []
        self.multi = multi


def MB(name=""):
    return Buf(name, multi=True)


class Op:
    __slots__ = ("eng", "fn", "is_dma", "deps", "signal", "ticket", "sem", "semval", "semprev", "idx")

    def __init__(self, eng, fn, is_dma):
        self.eng = eng
        self.fn = fn
        self.is_dma = is_dma
        self.deps = []
        self.signal = False
        self.ticket = None
        self.sem = None
        self.semval = None
        self.semprev = None


class Sched:
    def __init__(self, nc, n_dma_sems=16):
        self.nc = nc
        self.ops = []
        self.n_dma_sems = n_dma_sems
        self.last = {}
        self.recent_dma = {e: [] for e in ENGS}

    def _add(self, eng, fn, r, w, is_dma):
        op = Op(eng, fn, is_dma)
        op.idx = len(self.ops)
        deps = {}
        for b in r:
            for y in b.lws:
                deps[y.idx] = y
        for b in w:
            if not b.multi:
                for y in b.lws:
                    deps[y.idx] = y
            for e, y in b.rd.items():
                if y.eng == eng and not is_dma:
                    continue
                deps[y.idx] = y
            for y in b.rd_dma:
                deps[y.idx] = y
        for y in deps.values():
            if (not y.is_dma) and y.eng == "pe" and eng == "pe" and not is_dma:
                continue
            op.deps.append(y)
            y.signal = True
        for b in r:
            if is_dma:
                b.rd_dma.append(op)
            else:
                b.rd[eng] = op
        for b in w:
            if b.multi and not b.rd and not b.rd_dma:
                b.lws.append(op)
            else:
                b.lws = [op]
                b.rd = {}
                b.rd_dma = []
        self.ops.append(op)
        if is_dma:
            self.recent_dma[eng].append(op)
            if len(self.recent_dma[eng]) > self.n_dma_sems:
                self.recent_dma[eng].pop(0)
        else:
            self.last[eng] = op
        return op

    def barrier(self):
        B = Op("sp", lambda e: e.nop(), False)
        B.idx = len(self.ops)
        for y in self.last.values():
            if y.eng != "sp":
                B.deps.append(y)
                y.signal = True
        for q in self.recent_dma.values():
            B.deps.extend(q)
        B.signal = True
        self.ops.append(B)
        self.last["sp"] = B
        for e in ENGS:
            if e == "sp":
                continue
            o = Op(e, lambda en: en.nop(), False)
            o.idx = len(self.ops)
            o.deps.append(B)
            self.ops.append(o)
            self.last[e] = o

    def op(self, eng, fn, r=(), w=()):
        return self._add(eng, fn, r, w, False)

    def dma(self, eng, fn, r=(), w=()):
        return self._add(eng, fn, r, w, True)

    def emit(self, final_wait_eng="sp"):
        nc = self.nc
        with contextlib.ExitStack() as st:
            esem = {e: st.enter_context(nc.semaphore("es_" + e)) for e in ENGS}
            dsem = {e: [st.enter_context(nc.semaphore("ds_%s%d" % (e, i))) for i in range(self.n_dma_sems)]
                    for e in ("sp", "act", "pool")}
            cnt = {e: 0 for e in ENGS}
            dcnt = {e: 0 for e in ENGS}
            for op in self.ops:
                if op.is_dma:
                    k = dcnt[op.eng]
                    dcnt[op.eng] += 1
                    op.sem = dsem[op.eng][k % self.n_dma_sems]
                    op.semprev = 16 * (k // self.n_dma_sems)
                    op.semval = op.semprev + 16
                elif op.signal:
                    cnt[op.eng] += 1
                    op.ticket = cnt[op.eng]
            final = []
            for e in ("sp", "act", "pool"):
                n = dcnt[e]
                for i in range(min(n, self.n_dma_sems)):
                    last_k = ((n - 1 - i) // self.n_dma_sems) * self.n_dma_sems + i
                    final.append((dsem[e][i], 16 * (last_k // self.n_dma_sems) + 16))
            ops = self.ops
            blk = st.enter_context(nc.Block())

            def run(engname, eng):
                waited = {}
                for op in ops:
                    if op.eng != engname:
                        continue
                    for y in op.deps:
                        if y.is_dma:
                            key, val, sem = ("d", id(y.sem)), y.semval, y.sem
                        else:
                            key, val, sem = ("e", y.eng), y.ticket, esem[y.eng]
                        if waited.get(key, 0) >= val:
                            continue
                        waited[key] = val
                        eng.wait_ge(sem, val)
                    if op.is_dma:
                        key = ("d", id(op.sem))
                        if op.semprev > 0 and waited.get(key, 0) < op.semprev:
                            eng.wait_ge(op.sem, op.semprev)
                            waited[key] = op.semprev
                        ins = op.fn(eng)
                        ins.then_inc(op.sem, 16)
                    else:
                        ins = op.fn(eng)
                        if op.signal:
                            ins.then_inc(esem[engname], 1)
                if engname == final_wait_eng:
                    for sem, val in final:
                        eng.wait_ge(sem, val)
                    for e in ENGS:
                        if cnt[e] > 0:
                            eng.wait_ge(esem[e], cnt[e])

            @blk.sync
            def _(e):
                run("sp", e)

            @blk.scalar
            def _(e):
                run("act", e)

            @blk.tensor
            def _(e):
                run("pe", e)

            @blk.vector
            def _(e):
                run("dve", e)

            @blk.gpsimd
            def _(e):
                run("pool", e)


def I(name, *a, **k):
    return lambda e: getattr(e, name)(*a, **k)


def ap_of(t):
    return t[:] if not isinstance(t, bass.AP) else t


def mkap(base, off, dims):
    return bass.AP(base.tensor, base.offset + off, [list(d) for d in dims])


INPUT_SPECS = [
    ("xin", [TT, D]), ("cvec", [2, D]), ("state", [2, 2, 512]),
    ("cnk", [2, 4, 512, 64]), ("cnv", [2, 4, 512, 64]), ("cdk", [2, 4, 2, 512, 32]), ("cdv", [2, 4, 512, 64]),
    ("ada_w", [2, D, 6 * D]), ("ada_b", [2, 6 * D]), ("norm_mix", [2, D]), ("norm_ffn", [2, D]),
    ("w_in", [2, D, D_IN]), ("lru_conv_w", [2, 4, 512]), ("lru_conv_b", [2, 512]),
    ("lru_wa", [2, 2, 8, 64, 64]), ("lru_ba", [2, 2, 512]), ("lru_wx", [2, 2, 8, 64, 64]), ("lru_bx", [2, 2, 512]),
    ("lru_lambda", [2, 2, 512]), ("na_q_norm", [2, 64]), ("na_k_norm", [2, 64]), ("na_rpb", [2, 4, 15, 31]),
    ("df_q_norm", [2, 32]), ("df_k_norm", [2, 32]), ("df_lambda", [2, 4, 32]), ("df_subln", [2, 64]),
    ("w_out", [2, D, D]), ("ffn_up", [2, D, 2 * D_FF]), ("ffn_conv_w", [2, 3, 2 * D_FF]), ("ffn_conv_b", [2, 2 * D_FF]),
    ("ffn_down", [2, D_FF, D]),
    ("c_ident", [128, 128]), ("c_p60", [60, 60]), ("c_oh", [32, 4096]), ("c_ropec", [NS_T, 32]), ("c_ropes", [NS_T, 32]),
]
OUTPUT_SPECS = [
    ("y", [TT, D]), ("o_lru", [4, 2, 2, 512]), ("o_nk", [4, 2, 4, 256, 64]), ("o_nv", [4, 2, 4, 256, 64]),
    ("o_dk", [4, 2, 4, 2, 256, 32]), ("o_dv", [4, 2, 4, 256, 64]),
]
SCRATCH_SPECS = [
    ("MOD", [2, 2, 6 * D], F32), ("LX", [512, TT], F32), ("LG", [512, TT], BF16),
    ("NQT", [256, TT], BF16), ("NKT", [256, TT + L_CTX], BF16), ("NV", [TT + L_CTX, 256], BF16),
    ("DQT", [256, TT], BF16), ("DKT", [256, TT + L_CTX], BF16), ("DV", [TT + L_CTX, 256], BF16),
    ("CATT", [D, TT], BF16), ("XMID", [TT, D], F32), ("ACTT", [D_FF, TT], BF16), ("XRES", [TT, D], F32),
    ("ZR", [2, 60, 4096], BF16),
]


def build(stage=99, debug=False):
    nc = bass.Bass("TRN2", target_bir_lowering=False)
    din = {n: nc.dram_tensor(n, s, F32, kind="ExternalInput").ap() for n, s in INPUT_SPECS}
    dout = {n: nc.dram_tensor(n, s, F32, kind="ExternalOutput").ap() for n, s in OUTPUT_SPECS}
    scr = {n: nc.dram_tensor(n, s, dt, kind=("ExternalOutput" if debug else "Internal")).ap()
           for n, s, dt in SCRATCH_SPECS}
    with contextlib.ExitStack() as st:
        st.enter_context(nc.allow_non_contiguous_dma(reason="small strided parameter loads"))
        st.enter_context(nc.allow_low_precision(reason="bf16 matmul operands, fp32 accumulation"))
        S = Sched(nc)
        _emit_program(nc, st, S, din, dout, scr, stage)
        S.emit()
    return nc


def _emit_program(nc, st, S, din, dout, scr, stage):
    uid = [0]
    cur = [st]

    def sb(shape, dt=F32, name="t"):
        uid[0] += 1
        return cur[0].enter_context(nc.sbuf_tensor("%s_%d" % (name, uid[0]), shape, dt))

    @contextlib.contextmanager
    def phase():
        prev = cur[0]
        with contextlib.ExitStack() as s2:
            cur[0] = s2
            try:
                yield
            finally:
                cur[0] = prev
                S.barrier()
                ps_banks[0] = list(range(8))

    psum = [st.enter_context(nc.psum_tensor("ps%d" % i, [128, 512], F32)) for i in range(8)]
    psb = [Buf("ps%d" % i) for i in range(8)]
    psi = [0]

    ps_banks = [list(range(8))]

    def PS():
        bl = ps_banks[0]
        i = bl[psi[0] % len(bl)]
        psi[0] += 1
        return psum[i], psb[i]

    def pst(t):
        return ap_of(t).ap[0][0]

    ident = sb([128, 128], BF16, "ident")
    b_ident = Buf("ident")
    ones64 = sb([64, 64], F32, "ones64")
    b_ones64 = Buf()
    sel65 = sb([65, 64], F32, "sel65")
    b_sel65 = Buf()
    eps_t = sb([128, 1], F32, "eps")
    b_eps = Buf()
    with phase():
        ident_f = sb([128, 128], F32, "identf")
        bt = Buf()
        S.dma("sp", I("dma_start", out=ident_f[:], in_=din["c_ident"]), w=[bt])
        S.op("dve", I("tensor_copy", out=ident[:], in_=ident_f[:]), r=[bt], w=[b_ident])
    S.op("pool", I("memset", ones64[:], 1.0), w=[b_ones64])
    S.op("pool", I("memset", sel65[:], 0.0), w=[b_sel65])
    S.op("pool", I("memset", sel65[64:65, :], 1.0), w=[b_sel65])
    S.op("pool", I("memset", eps_t[:], EPS), w=[b_eps])

    def bcast_row(dram_row_ap, n, parts=128):
        return bass.AP(dram_row_ap.tensor, dram_row_ap.offset, [[0, parts], [1, n]])

    bMOD = MB("MOD")
    bLX = [MB() for _ in range(NTILE)]
    bLG = [MB() for _ in range(NTILE)]
    bNQT = [MB() for _ in range(NSUB)]
    bNKT = [MB() for _ in range(NSUB + 4)]
    bNV = [MB() for _ in range(NSUB + 4)]
    bDQT = [MB() for _ in range(NSUB)]
    bDKT = [MB() for _ in range(NSUB + 4)]
    bDV = [MB() for _ in range(NSUB + 4)]
    bCATT = [[MB() for _ in range(NTILE)] for _ in range(16)]
    bXMID = [MB() for _ in range(NSUB)]
    bACTT = [[MB() for _ in range(NTILE)] for _ in range(22)]
    bXRES = [MB() for _ in range(NSUB)]
    bZR = MB("ZRunused")

    LP = []
    for _l in range(2):
        LP.append(dict(prm=sb([128, 4, 11], F32, "prm"), b_prm=MB(), h0t=sb([128, 4, 2], F32, "h0t"), b_h0=MB(),
                       c8=sb([128, 4, 2], F32, "c8"), b_c8=Buf(), c8b=sb([128, 4, 2], F32, "c8b"), wbd=sb([128, 16, 128], BF16, "wbd"), b_wbd=Buf(),
                       nlam=sb([128, 1], F32, "nlam"), b_nlam=Buf(), gsub=sb([64, 1], F32, "gsub"), b_gsub=Buf()))
    bZRl = [MB("ZR0"), MB("ZR1")]

    def prep_layer(l, lam_init):
        P_ = LP[l]
        prm, b_prm, h0t, b_h0, c8, b_c8, wbd, b_wbd = P_["prm"], P_["b_prm"], P_["h0t"], P_["b_h0"], P_["c8"], P_["b_c8"], P_["wbd"], P_["b_wbd"]
        pa = prm[:]
        def pdst(off, n):
            return mkap(pa, off, [[pst(prm), 128], [11, 4], [1, n]])
        for c in range(4):
            src = din["lru_conv_w"]
            S.dma("sp", I("dma_start", out=prm[:, c, 0:4], in_=bass.AP(src.tensor, src.offset + l * 4 * 512 + c * 128, [[1, 128], [512, 4]])), w=[b_prm])
            src = din["lru_conv_b"]
            S.dma("sp", I("dma_start", out=prm[:, c, 4:5], in_=bass.AP(src.tensor, src.offset + l * 512 + c * 128, [[1, 128], [1, 1]])), w=[b_prm])
            for nm, off in (("lru_ba", 5), ("lru_bx", 7), ("lru_lambda", 9)):
                src = din[nm]
                S.dma("sp", I("dma_start", out=prm[:, c, off:off + 2], in_=bass.AP(src.tensor, src.offset + l * 1024 + c * 128, [[1, 128], [512, 2]])), w=[b_prm])
            src = din["state"]
            S.dma("sp", I("dma_start", out=h0t[:, c, :], in_=bass.AP(src.tensor, src.offset + l * 1024 + c * 128, [[1, 128], [512, 2]])), w=[b_h0])
        za = sb([128, 8], F32, "za"); zw = sb([128, 8], F32, "zw"); zs = sb([128, 8], F32, "zs"); z2 = sb([128, 8], F32, "z2"); zp = sb([128, 8], F32, "zp")
        b_z = Buf()
        lam_ap = mkap(pa, 9, [[pst(prm), 128], [11, 4], [1, 2]])
        za3 = za[:].rearrange("p (c d) -> p c d", d=2)
        S.op("dve", I("tensor_scalar", out=za3, in0=lam_ap, scalar1=-1.0, scalar2=None, op0=ALU.mult), r=[b_prm], w=[b_z])
        S.op("dve", I("tensor_tensor", out=zw[:].rearrange("p (c d) -> p c d", d=2), in0=za3, in1=lam_ap, op=ALU.max), r=[b_z, b_prm], w=[b_z])
        S.op("act", I("activation", out=zw[:], in_=zw[:], func=AF.Exp, scale=-1.0), r=[b_z], w=[b_z])
        S.op("dve", I("tensor_scalar", out=zs[:], in0=zw[:], scalar1=2.0, scalar2=None, op0=ALU.add), r=[b_z], w=[b_z])
        S.op("dve", I("reciprocal", out=zs[:], in_=zs[:]), r=[b_z], w=[b_z])
        S.op("dve", I("tensor_tensor", out=zs[:], in0=zs[:], in1=zw[:], op=ALU.mult), r=[b_z], w=[b_z])
        S.op("dve", I("tensor_tensor", out=z2[:], in0=zs[:], in1=zs[:], op=ALU.mult), r=[b_z], w=[b_z])
        S.op("dve", I("tensor_scalar", out=zp[:], in0=z2[:], scalar1=1.0 / 9, scalar2=1.0 / 7, op0=ALU.mult, op1=ALU.add), r=[b_z], w=[b_z])
        for cf in (1.0 / 5, 1.0 / 3, 1.0):
            S.op("dve", I("tensor_tensor", out=zp[:], in0=zp[:], in1=z2[:], op=ALU.mult), r=[b_z], w=[b_z])
            S.op("dve", I("tensor_scalar", out=zp[:], in0=zp[:], scalar1=cf, scalar2=None, op0=ALU.add), r=[b_z], w=[b_z])
        S.op("dve", I("tensor_tensor", out=zp[:], in0=zp[:], in1=zs[:], op=ALU.mult), r=[b_z], w=[b_z])
        S.op("dve", I("tensor_scalar", out=za[:], in0=za[:], scalar1=0.0, scalar2=0.5, op0=ALU.max, op1=ALU.mult), r=[b_z], w=[b_z])
        S.op("dve", I("tensor_tensor", out=zp[:], in0=zp[:], in1=za[:], op=ALU.add), r=[b_z], w=[b_z])
        S.op("dve", I("tensor_scalar", out=c8[:].rearrange("p c d -> p (c d)"), in0=zp[:], scalar1=-16.0, scalar2=None, op0=ALU.mult), r=[b_z], w=[b_c8])
        S.op("dve", I("tensor_scalar", out=P_["c8b"][:].rearrange("p c d -> p (c d)"), in0=zp[:], scalar1=-32.0, scalar2=None, op0=ALU.mult), r=[b_z], w=[b_c8])
        wbd_f = sb([128, 16, 128], F32, "wbdf"); b_wbdf = MB(); b_wbdz = Buf()
        S.op("pool", I("memset", wbd_f[:], 0.0), w=[b_wbdf, b_wbdz])
        for c in range(4):
            for dr in range(2):
                for gate, nm in enumerate(("lru_wa", "lru_wx")):
                    gi = (c * 2 + dr) * 2 + gate
                    for half in range(2):
                        S.dma("sp", I("dma_start", out=wbd_f[half * 64:(half + 1) * 64, gi, half * 64:(half + 1) * 64], in_=din[nm][l, dr, 2 * c + half]), r=[b_wbdz], w=[b_wbdf])
        S.op("dve", I("tensor_copy", out=wbd[:], in_=wbd_f[:]), r=[b_wbdf], w=[b_wbd])
        dl = sb([128, 4, 32], F32, "dl"); b_dl = Buf()
        S.dma("sp", I("dma_start", out=dl[:].rearrange("p a b -> p (a b)"), in_=bass.AP(din["df_lambda"].tensor, din["df_lambda"].offset + l * 128, [[0, 128], [1, 128]])), w=[b_dl])
        pr = sb([128, 2, 32], F32, "pr"); b_pr = Buf()
        d4 = dl[:].rearrange("p (a two) b -> p a two b", two=2)
        S.op("dve", I("tensor_tensor", out=pr[:], in0=d4[:, :, 0, :], in1=d4[:, :, 1, :], op=ALU.mult), r=[b_dl], w=[b_pr])
        sm = sb([128, 2], F32, "sm"); b_sm = Buf()
        S.op("dve", I("tensor_reduce", out=sm[:], in_=pr[:], axis=AX.X, op=ALU.add), r=[b_pr], w=[b_sm])
        S.op("act", I("activation", out=sm[:], in_=sm[:], func=AF.Exp), r=[b_sm], w=[b_sm])
        nlam, b_nlam, gsub, b_gsub = LP[l]["nlam"], LP[l]["b_nlam"], LP[l]["gsub"], LP[l]["b_gsub"]
        bZR = bZRl[l]
        S.op("dve", I("tensor_tensor", out=nlam[:], in0=sm[:, 1:2], in1=sm[:, 0:1], op=ALU.subtract), r=[b_sm], w=[b_nlam])
        S.op("dve", I("tensor_scalar", out=nlam[:], in0=nlam[:], scalar1=-lam_init, scalar2=None, op0=ALU.add), r=[b_nlam], w=[b_nlam])
        src = din["df_subln"]
        S.dma("sp", I("dma_start", out=gsub[:], in_=bass.AP(src.tensor, src.offset + l * 64, [[1, 64], [1, 1]])), w=[b_gsub])
        S.op("dve", I("tensor_scalar", out=gsub[:], in0=gsub[:], scalar1=1.0 - lam_init, scalar2=None, op0=ALU.mult), r=[b_gsub], w=[b_gsub])
        if True:
            rpa = sb([60, 32], F32, "rpa"); b_rpa = Buf()
            S.op("pool", I("memset", rpa[:], 1.0), w=[b_rpa])
            S.dma("sp", I("dma_start", out=rpa[:, 0:31], in_=din["na_rpb"][l].rearrange("h r c -> (h r) c")), w=[b_rpa])
            p60 = sb([60, 60], F32, "p60"); b_p60 = Buf()
            S.dma("sp", I("dma_start", out=p60[:], in_=din["c_p60"]), w=[b_p60])
            oh = sb([32, 4096], F32, "oh"); b_oh = Buf()
            S.dma("sp", I("dma_start", out=oh[:], in_=din["c_oh"]), w=[b_oh])
            ps, pb = PS()
            S.op("pe", I("matmul", ps[0:32, 0:60], lhsT=rpa[:], rhs=p60[:], start=True, stop=True), r=[b_rpa, b_p60], w=[pb])
            l2 = sb([32, 60], F32, "l2"); b_l2 = Buf()
            S.op("act", I("activation", out=l2[:], in_=ps[0:32, 0:60], func=AF.Copy), r=[pb], w=[b_l2])
            zs = [sb([60, 512], BF16, "zs") for _ in range(2)]; b_zs = [Buf(), Buf()]
            for nb in range(8):
                ps, pb = PS()
                S.op("pe", I("matmul", ps[0:60, :], lhsT=l2[:], rhs=oh[:, nb * 512:(nb + 1) * 512], start=True, stop=True), r=[b_l2, b_oh], w=[pb])
                S.op("act", I("activation", out=zs[nb % 2][:], in_=ps[0:60, :], func=AF.Exp), r=[pb], w=[b_zs[nb % 2]])
                S.dma("sp", I("dma_start", out=scr["ZR"][l][:, nb * 512:(nb + 1) * 512], in_=zs[nb % 2][:]), r=[b_zs[nb % 2]], w=[bZR])


    with phase():
        for _l in range(2):
            prep_layer(_l, 0.8 - 0.6 * float(np.exp(-0.3 * _l)))
        cT = sb([128, 8, 2], F32, "cT")
        b_cT = Buf()
        for g in range(2):
            S.dma("sp", I("dma_start", out=cT[:, :, g], in_=din["cvec"][g].rearrange("(k p) -> p k", p=128)), w=[b_cT])
        S.op("act", I("activation", out=cT[:], in_=cT[:], func=AF.Silu), r=[b_cT], w=[b_cT])
        adab = sb([2, 2, 6 * D], F32, "adab")
        modsb = [sb([2, 512], F32, "modsb") for _ in range(2)]
        aw = [sb([128, 8, 512], F32, "aw%d" % i) for i in range(3)]
        b_aw = [Buf(), Buf(), Buf()]
        b_adab = MB()
        b_modsb = [Buf(), Buf()]
        for l in range(2):
            S.dma("sp", I("dma_start", out=adab[:, l, :], in_=bass.AP(din["ada_b"].tensor, din["ada_b"].offset + l * 6 * D, [[0, 2], [1, 6 * D]])), w=[b_adab])
        def aw_load(it):
            l_, nb_ = it // 12, it % 12
            S.dma("sp", I("dma_start", out=aw[it % 3][:], in_=din["ada_w"][l_, :, nb_ * 512:(nb_ + 1) * 512].rearrange("(k p) n -> p k n", p=128)), w=[b_aw[it % 3]])

        for it in range(3):
            aw_load(it)
        for it in range(24):
            l, nb = it // 12, it % 12
            k = it % 3
            ps, pb = PS()
            for kc in range(8):
                S.op("pe", I("matmul", ps[0:2, :], lhsT=cT[:, kc, :], rhs=aw[k][:, kc, :], start=(kc == 0), stop=(kc == 7)),
                     r=[b_cT, b_aw[k]], w=[pb])
            m_ = it % 2
            S.op("dve", I("tensor_tensor", out=modsb[m_][:], in0=ps[0:2, :], in1=adab[:, l, nb * 512:(nb + 1) * 512], op=ALU.add), r=[pb, b_adab], w=[b_modsb[m_]])
            S.dma("sp", I("dma_start", out=scr["MOD"][l, :, nb * 512:(nb + 1) * 512], in_=modsb[m_][:]), r=[b_modsb[m_]], w=[bMOD])
            if it + 3 < 24:
                aw_load(it + 3)
    if stage <= 0:
        return

    mpool = [sb([128, D], F32, "mod%d" % i) for i in range(6)]
    mpool_b = [Buf() for _ in range(6)]
    gtmp = sb([128, D], F32, "gtmp")
    b_gtmp = Buf()

    def mod_row(l, g, j):
        m = scr["MOD"]
        return bass.AP(m.tensor, m.offset + ((l * 2 + g) * 6 + j) * D, [[0, 128], [1, D]])

    def load_mod(l, g, j, slot):
        S.dma("sp", I("dma_start", out=mpool[slot][:], in_=mod_row(l, g, j)), r=[bMOD], w=[mpool_b[slot]])

    def load_gs(l, g, j_scale, norm_name, slot):
        nrm = din[norm_name]
        S.dma("sp", I("dma_start", out=gtmp[:], in_=bass.AP(nrm.tensor, nrm.offset + l * D, [[0, 128], [1, D]])), w=[b_gtmp])
        load_mod(l, g, j_scale, slot)
        S.op("dve", I("scalar_tensor_tensor", out=mpool[slot][:], in0=mpool[slot][:], scalar=1.0, in1=gtmp[:], op0=ALU.add, op1=ALU.mult),
             r=[mpool_b[slot], b_gtmp], w=[mpool_b[slot]])

    def mk_wk():
        return dict(sq=sb([128, D], BF16, "sq"), ss=sb([128, 1], F32, "ss"), t1=sb([128, D], F32, "t1"), hb=sb([128, D], BF16, "hb"),
                    b_sq=Buf(), b_ss=Buf(), b_t1=Buf(), b_hb=Buf())

    def pipeline(n, stages, skew=1):
        ns = len(stages)
        for t in range(n + (ns - 1) * skew):
            for si, f in enumerate(stages):
                i = t - si * skew
                if 0 <= i < n:
                    f(i)

    def rms_mod_transpose(xt, bx, j, GS, SH, bGS, bSH, wk, hT, b_hT):
        rms_mod(xt, bx, GS, SH, bGS, bSH, wk)
        transpose_to_hT(j, wk, hT, b_hT)

    def rms_mod(xt, bx, GS, SH, bGS, bSH, wk):
        sq, ss, t1, hb = wk["sq"], wk["ss"], wk["t1"], wk["hb"]
        S.op("act", I("activation", out=sq[:], in_=xt[:], func=AF.Square, accum_out=ss[:]), r=[bx], w=[wk["b_sq"], wk["b_ss"]])
        S.op("act", I("activation", out=ss[:], in_=ss[:], func=AF.Sqrt, scale=1.0 / D, bias=eps_t[:, 0:1]), r=[wk["b_ss"], b_eps], w=[wk["b_ss"]])
        S.op("dve", I("reciprocal", out=ss[:], in_=ss[:]), r=[wk["b_ss"]], w=[wk["b_ss"]])
        S.op("dve", I("scalar_tensor_tensor", out=t1[:], in0=xt[:], scalar=ss[:, 0:1], in1=GS[:], op0=ALU.mult, op1=ALU.mult),
             r=[bx, wk["b_ss"], bGS], w=[wk["b_t1"]])
        S.op("dve", I("tensor_tensor", out=hb[:], in0=t1[:], in1=SH[:], op=ALU.add), r=[wk["b_t1"], bSH], w=[wk["b_hb"]])

    def transpose_to_hT(j, wk, hT, b_hT):
        hb = wk["hb"]
        for half in range(2):
            ps, pb = PS()
            for q in range(4):
                c = half * 4 + q
                S.op("pe", I("matmul", ps[:, q * 128:(q + 1) * 128], lhsT=hb[:, c * 128:(c + 1) * 128], rhs=ident[:], start=True, stop=True),
                     r=[wk["b_hb"], b_ident], w=[pb])
            dst = hT[:, half * 4:(half + 1) * 4, j * 128:(j + 1) * 128]
            src = ps[:].rearrange("p (q n) -> p q n", q=4)
            S.op("act", I("activation", out=dst, in_=src, func=AF.Copy), r=[pb], w=[b_hT[j]])

    wcnt = [0]

    def p1_phase(l, x_src, bx_src):
        hT = sb([128, 8, TT], BF16, "hT")
        b_hT = [Buf() for _ in range(NSUB)]
        wks = [mk_wk(), mk_wk()]
        xts = [sb([128, D], F32, "xt") for _ in range(2)]
        bxts = [Buf() for _ in range(2)]
        NSET = 3
        tsets = []
        for _i in range(NSET):
            tsets.append((sb([128, 512], F32, "qk"), Buf(), sb([128, 512], F32, "sq2"), Buf(), sb([128, 16], F32, "ss8"), Buf(),
                          sb([128, 512], F32, "n1"), Buf(), sb([128, 512], F32, "n2"), Buf(), None, None,
                          sb([128, 512], BF16, "nb16"), Buf()))
        qk, b_qk, sq2, b_sq2, ss8, b_ss8, n1, b_n1, n2, b_n2, n3, b_n3, nb16, b_nb16 = tsets[0]
        rotc = [0]

        def rot():
            nonlocal qk, b_qk, sq2, b_sq2, ss8, b_ss8, n1, b_n1, n2, b_n2, n3, b_n3, nb16, b_nb16
            rotc[0] += 1
            qk, b_qk, sq2, b_sq2, ss8, b_ss8, n1, b_n1, n2, b_n2, n3, b_n3, nb16, b_nb16 = tsets[rotc[0] % NSET]
        tb16 = [sb([128, 512], BF16, "tb16") for _ in range(2)]; b_tb16 = [Buf(), Buf()]
        vb16 = [sb([128, 256], BF16, "vb16") for _ in range(2)]; b_vb16 = [Buf(), Buf()]
        vf32 = [sb([128, 256], F32, "vf32") for _ in range(2)]; b_vf32 = [Buf(), Buf()]
        stg_f = [sb([128, 512], F32, "stgf") for _ in range(2)]; b_stg_f = [Buf(), Buf()]
        stg_b = [sb([128, 512], BF16, "stgb") for _ in range(2)]; b_stg_b = [Buf(), Buf()]
        gqk = sb([128, 512], F32, "gqk"); b_gqk = MB()
        gdq = sb([128, 256], F32, "gdq"); b_gdq = MB()
        gdk = sb([128, 256], F32, "gdk"); b_gdk = MB()
        ropeC = sb([128, 32, 32], F32, "ropeC"); ropeS = sb([128, 32, 32], F32, "ropeS"); b_rope = MB()
        S.dma("sp", I("dma_start", out=ropeC[:], in_=din["c_ropec"].rearrange("(s p) f -> p s f", p=128)), w=[b_rope])
        S.dma("sp", I("dma_start", out=ropeS[:], in_=din["c_ropes"].rearrange("(s p) f -> p s f", p=128)), w=[b_rope])
        wcg = [sb([128, 8, 512], BF16, "wcg") for _ in range(2)]; b_wcg = [Buf(), Buf()]

        def load_w_cols(dram_w2d, c0, ncols):
            k = wcnt[0] % 2
            wcnt[0] += 1
            S.dma("pool", I("dma_start", out=wcg[k][:, :, 0:ncols], in_=dram_w2d[:, c0:c0 + ncols].rearrange("(k p) n -> p k n", p=128)), w=[b_wcg[k]])
            return wcg[k], b_wcg[k]

        def headnorm(src_ap, src_b, ngrp, gsz, gain_ap, gain_b, out_t, out_b):
            n = ngrp * gsz
            S.op("dve", I("tensor_tensor", out=sq2[:, 0:n], in0=src_ap, in1=src_ap, op=ALU.mult), r=[src_b], w=[b_sq2])
            S.op("dve", I("tensor_reduce", out=ss8[:, 0:ngrp], in_=sq2[:, 0:n].rearrange("p (g d) -> p g d", d=gsz), axis=AX.X, op=ALU.add),
                 r=[b_sq2], w=[b_ss8])
            S.op("act", I("activation", out=ss8[:, 0:ngrp], in_=ss8[:, 0:ngrp], func=AF.Sqrt, scale=1.0 / gsz, bias=eps_t[:, 0:1]), r=[b_ss8, b_eps], w=[b_ss8])
            S.op("dve", I("reciprocal", out=ss8[:, 0:ngrp], in_=ss8[:, 0:ngrp]), r=[b_ss8], w=[b_ss8])
            rb = mkap(ss8[:], 0, [[pst(ss8), 128], [1, ngrp], [0, gsz]])
            S.op("dve", I("tensor_tensor", out=n1[:, 0:n].rearrange("p (g d) -> p g d", d=gsz), in0=src_ap.rearrange("p (g d) -> p g d", d=gsz), in1=rb, op=ALU.mult),
                 r=[src_b, b_ss8], w=[b_n1])
            S.op("dve", I("tensor_tensor", out=out_t, in0=n1[:, 0:n], in1=gain_ap, op=ALU.mult), r=[b_n1, gain_b], w=[out_b])

        def rope(src_t, src_b, dst_t, dst_b, sub):
            C = mkap(ropeC[:], sub * 32, [[pst(ropeC), 128], [0, 8], [1, 32]])
            S.op("dve", I("tensor_tensor", out=n1[:, 0:256].rearrange("p (g d) -> p g d", d=32), in0=src_t.rearrange("p (g d) -> p g d", d=32), in1=C, op=ALU.mult),
                 r=[src_b, b_rope], w=[b_n1])
            for hf in range(2):
                in0 = mkap(src_t, (1 - hf) * 8, [[src_t.ap[0][0], 128], [32, 8], [16, 2], [1, 8]])
                in1 = mkap(ropeS[:], sub * 32 + hf * 8, [[pst(ropeS), 128], [0, 8], [16, 2], [1, 8]])
                o = mkap(sq2[:], hf * 8, [[pst(sq2), 128], [32, 8], [16, 2], [1, 8]])
                S.op("dve", I("tensor_tensor", out=o, in0=in0, in1=in1, op=ALU.mult), r=[src_b, b_rope], w=[b_sq2])
            S.op("dve", I("tensor_tensor", out=dst_t, in0=n1[:, 0:256], in1=sq2[:, 0:256], op=ALU.add), r=[b_n1, b_sq2], w=[dst_b])

        def transpose_store(src16, src_b, nblk, dsts, k):
            ps, pb = PS()
            for q in range(nblk):
                S.op("pe", I("matmul", ps[:, q * 128:(q + 1) * 128], lhsT=src16[:, q * 128:(q + 1) * 128], rhs=ident[:], start=True, stop=True),
                     r=[src_b, b_ident], w=[pb])
            S.op("act", I("activation", out=tb16[k][:, 0:nblk * 128], in_=ps[:, 0:nblk * 128], func=AF.Copy), r=[pb], w=[b_tb16[k]])
            for q in range(nblk):
                dap, dbuf = dsts[q]
                S.dma("sp", I("dma_start", out=dap, in_=tb16[k][:, q * 128:(q + 1) * 128]), r=[b_tb16[k]], w=[dbuf])

        for g in range(2):
            load_gs(l, g, 1, "norm_mix", 2 * g)
            load_mod(l, g, 0, 2 * g + 1)
        for h in range(4):
            S.dma("sp", I("dma_start", out=gqk[:, h * 64:(h + 1) * 64], in_=bcast_row(din["na_q_norm"][l], 64)), w=[b_gqk])
            S.dma("sp", I("dma_start", out=gqk[:, 256 + h * 64:256 + (h + 1) * 64], in_=bcast_row(din["na_k_norm"][l], 64)), w=[b_gqk])
        S.op("dve", I("tensor_scalar", out=gqk[:, 0:256], in0=gqk[:, 0:256], scalar1=64 ** -0.5, scalar2=None, op0=ALU.mult), r=[b_gqk], w=[b_gqk])
        for gi in range(8):
            S.dma("sp", I("dma_start", out=gdq[:, gi * 32:(gi + 1) * 32], in_=bcast_row(din["df_q_norm"][l], 32)), w=[b_gdq])
            S.dma("sp", I("dma_start", out=gdk[:, gi * 32:(gi + 1) * 32], in_=bcast_row(din["df_k_norm"][l], 32)), w=[b_gdk])
        S.op("dve", I("tensor_scalar", out=gdq[:], in0=gdq[:], scalar1=32 ** -0.5, scalar2=None, op0=ALU.mult), r=[b_gdq], w=[b_gdq])

        def p1a_A(j):
            g = 0 if j < 8 else 1
            xt, bx = xts[j % 2], bxts[j % 2]
            S.dma("sp", I("dma_start", out=xt[:], in_=x_src[j * 128:(j + 1) * 128, :]), r=[bx_src[j]], w=[bx])
            rms_mod(xt, bx, mpool[2 * g], mpool[2 * g + 1], mpool_b[2 * g], mpool_b[2 * g + 1], wks[j % 2])

        def p1a_B(j):
            transpose_to_hT(j, wks[j % 2], hT, b_hT)

        pipeline(NSUB, [p1a_A, p1a_B])

        if stage <= 0.5:
            return
        w2d = din["w_in"][l]
        wq = [load_w_cols(w2d, 0, 512)]
        for cg in range(2):
            W, bW = wq.pop(0)
            wq.append(load_w_cols(w2d, (cg + 1) * 512, 512))
            for t in range(NTILE):
                for oc in range(4):
                    ps, pb = PS()
                    for kc in range(8):
                        S.op("pe", I("matmul", ps[:], lhsT=W[:, kc, oc * 128:(oc + 1) * 128], rhs=hT[:, kc, t * 512:(t + 1) * 512], start=(kc == 0), stop=(kc == 7)),
                             r=[bW] + b_hT[t * 4:(t + 1) * 4], w=[pb])
                    k = oc % 2
                    if cg == 0:
                        S.op("act", I("activation", out=stg_f[k][:], in_=ps[:], func=AF.Copy), r=[pb], w=[b_stg_f[k]])
                        S.dma("sp", I("dma_start", out=scr["LX"][oc * 128:(oc + 1) * 128, t * 512:(t + 1) * 512], in_=stg_f[k][:]), r=[b_stg_f[k]], w=[bLX[t]])
                    else:
                        S.op("act", I("activation", out=stg_b[k][:], in_=ps[:], func=AF.Gelu_apprx_tanh), r=[pb], w=[b_stg_b[k]])
                        S.dma("sp", I("dma_start", out=scr["LG"][oc * 128:(oc + 1) * 128, t * 512:(t + 1) * 512], in_=stg_b[k][:]), r=[b_stg_b[k]], w=[bLG[t]])
        if stage <= 0.6:
            return
        W, bW = wq.pop(0)
        wq.append(load_w_cols(w2d, 1536, 512))
        ctxs = {}

        def cg2_A(j, W=W, bW=bW):
            rot()
            ps, pb = PS()
            for kc in range(8):
                S.op("pe", I("matmul", ps[:], lhsT=hT[:, kc, j * 128:(j + 1) * 128], rhs=W[:, kc, :], start=(kc == 0), stop=(kc == 7)), r=[bW, b_hT[j]], w=[pb])
            S.op("act", I("activation", out=qk[:], in_=ps[:], func=AF.Copy), r=[pb], w=[b_qk])
            headnorm(qk[:], b_qk, 8, 64, gqk[:], b_gqk, n2[:], b_n2)
            if j < 8:
                b_, t0 = j // 2, (j % 2) * 128
                o = dout["o_nk"]
                dst = bass.AP(o.tensor, o.offset + ((b_ * 2 + l) * 4 * 256 + t0) * 64, [[64, 128], [256 * 64, 4], [1, 64]])
                S.dma("sp", I("dma_start", out=dst, in_=n2[:, 256:512].rearrange("p (h d) -> p h d", d=64)), r=[b_n2])
            S.op("dve", I("tensor_copy", out=nb16[:], in_=n2[:]), r=[b_n2], w=[b_nb16])
            ctxs[j] = (nb16, b_nb16)

        def cg2_B(j):
            nb, bnb = ctxs.pop(j)
            dsts = [(scr["NQT"][q * 128:(q + 1) * 128, j * 128:(j + 1) * 128], bNQT[j]) for q in range(2)] + \
                   [(scr["NKT"][q * 128:(q + 1) * 128, j * 128:(j + 1) * 128], bNKT[j]) for q in range(2)]
            transpose_store(nb, bnb, 4, dsts, j % 2)

        pipeline(NSUB, [cg2_A, cg2_B], skew=2)
        if stage <= 0.7:
            return
        W, bW = wq.pop(0)
        wq.append(load_w_cols(w2d, 2048, 512))
        def cg3_A(j, W=W, bW=bW):
            k = j % 2
            rot()
            ps, pb = PS()
            for kc in range(8):
                S.op("pe", I("matmul", ps[:], lhsT=hT[:, kc, j * 128:(j + 1) * 128], rhs=W[:, kc, :], start=(kc == 0), stop=(kc == 7)), r=[bW, b_hT[j]], w=[pb])
            S.op("act", I("activation", out=vb16[k][:], in_=ps[:, 0:256], func=AF.Copy), r=[pb], w=[b_vb16[k]])
            S.dma("sp", I("dma_start", out=scr["NV"][j * 128:(j + 1) * 128, :], in_=vb16[k][:]), r=[b_vb16[k]], w=[bNV[j]])
            if j < 8:
                b_, t0 = j // 2, (j % 2) * 128
                S.op("act", I("activation", out=vf32[k][:], in_=ps[:, 0:256], func=AF.Copy), r=[pb], w=[b_vf32[k]])
                o = dout["o_nv"]
                dst = bass.AP(o.tensor, o.offset + ((b_ * 2 + l) * 4 * 256 + t0) * 64, [[64, 128], [256 * 64, 4], [1, 64]])
                S.dma("sp", I("dma_start", out=dst, in_=vf32[k][:].rearrange("p (h d) -> p h d", d=64)), r=[b_vf32[k]])
            S.op("act", I("activation", out=qk[:, 0:256], in_=ps[:, 256:512], func=AF.Copy), r=[pb], w=[b_qk])
            headnorm(qk[:, 0:256], b_qk, 8, 32, gdq[:], b_gdq, n2[:, 0:256], b_n2)
            if j >= 8:
                rope(n2[:, 0:256], b_n2, nb16[:, 0:256], b_nb16, j - 8)
            else:
                S.op("dve", I("tensor_copy", out=nb16[:, 0:256], in_=n2[:, 0:256]), r=[b_n2], w=[b_nb16])
            ctxs[j] = (nb16, b_nb16)

        def cg3_B(j):
            nb, bnb = ctxs.pop(j)
            dsts = [(scr["DQT"][q * 128:(q + 1) * 128, j * 128:(j + 1) * 128], bDQT[j]) for q in range(2)]
            transpose_store(nb, bnb, 2, dsts, j % 2)

        pipeline(NSUB, [cg3_A, cg3_B], skew=2)
        if stage <= 0.8:
            return
        W, bW = wq.pop(0)
        def cg4_A(j, W=W, bW=bW):
            k = j % 2
            rot()
            ps, pb = PS()
            for kc in range(8):
                S.op("pe", I("matmul", ps[:], lhsT=hT[:, kc, j * 128:(j + 1) * 128], rhs=W[:, kc, :], start=(kc == 0), stop=(kc == 7)), r=[bW, b_hT[j]], w=[pb])
            S.op("act", I("activation", out=vb16[k][:], in_=ps[:, 256:512], func=AF.Copy), r=[pb], w=[b_vb16[k]])
            S.dma("sp", I("dma_start", out=scr["DV"][j * 128:(j + 1) * 128, :], in_=vb16[k][:]), r=[b_vb16[k]], w=[bDV[j]])
            if j < 8:
                b_, t0 = j // 2, (j % 2) * 128
                S.op("act", I("activation", out=vf32[k][:], in_=ps[:, 256:512], func=AF.Copy), r=[pb], w=[b_vf32[k]])
                o = dout["o_dv"]
                dst = bass.AP(o.tensor, o.offset + ((b_ * 2 + l) * 4 * 256 + t0) * 64, [[64, 128], [256 * 64, 4], [1, 64]])
                S.dma("sp", I("dma_start", out=dst, in_=vf32[k][:].rearrange("p (h d) -> p h d", d=64)), r=[b_vf32[k]])
            S.op("act", I("activation", out=qk[:, 0:256], in_=ps[:, 0:256], func=AF.Copy), r=[pb], w=[b_qk])
            headnorm(qk[:, 0:256], b_qk, 8, 32, gdk[:], b_gdk, n2[:, 0:256], b_n2)
            if j >= 8:
                rope(n2[:, 0:256], b_n2, nb16[:, 0:256], b_nb16, j - 8)
            else:
                b_, t0 = j // 2, (j % 2) * 128
                o = dout["o_dk"]
                dst = bass.AP(o.tensor, o.offset + ((b_ * 2 + l) * 8 * 256 + t0) * 32, [[32, 128], [256 * 32, 8], [1, 32]])
                S.dma("sp", I("dma_start", out=dst, in_=n2[:, 0:256].rearrange("p (g d) -> p g d", d=32)), r=[b_n2])
                S.op("dve", I("tensor_copy", out=nb16[:, 0:256], in_=n2[:, 0:256]), r=[b_n2], w=[b_nb16])
            ctxs[j] = (nb16, b_nb16)

        def cg4_B(j):
            nb, bnb = ctxs.pop(j)
            dsts = [(scr["DKT"][q * 128:(q + 1) * 128, j * 128:(j + 1) * 128], bDKT[j]) for q in range(2)]
            transpose_store(nb, bnb, 2, dsts, j % 2)

        pipeline(NSUB, [cg4_A, cg4_B], skew=2)
        if stage <= 0.9:
            return
        for cch in range(4):
            jj = NSUB + cch
            k = cch % 2
            rot()
            col0 = TT + cch * 128
            src = din["cnk"]
            S.dma("sp", I("dma_start", out=qk[:, 0:256].rearrange("p (h d) -> p h d", d=64),
                          in_=bass.AP(src.tensor, src.offset + (l * 4 * 512 + cch * 128) * 64, [[64, 128], [512 * 64, 4], [1, 64]])), w=[b_qk])
            S.op("dve", I("tensor_copy", out=nb16[:, 0:256], in_=qk[:, 0:256]), r=[b_qk], w=[b_nb16])
            transpose_store(nb16, b_nb16, 2, [(scr["NKT"][q * 128:(q + 1) * 128, col0:col0 + 128], bNKT[jj]) for q in range(2)], k)
            src = din["cdk"]
            S.dma("sp", I("dma_start", out=qk[:, 256:512].rearrange("p (g d) -> p g d", d=32),
                          in_=bass.AP(src.tensor, src.offset + (l * 8 * 512 + cch * 128) * 32, [[32, 128], [512 * 32, 8], [1, 32]])), w=[b_qk])
            S.op("dve", I("tensor_copy", out=nb16[:, 256:512], in_=qk[:, 256:512]), r=[b_qk], w=[b_nb16])
            transpose_store(nb16[:, 256:512], b_nb16, 2, [(scr["DKT"][q * 128:(q + 1) * 128, col0:col0 + 128], bDKT[jj]) for q in range(2)], 1 - k)
            for nm, dst, bb in (("cnv", "NV", bNV), ("cdv", "DV", bDV)):
                src = din[nm]
                S.dma("sp", I("dma_start", out=n2[:, 0:256].rearrange("p (h d) -> p h d", d=64),
                              in_=bass.AP(src.tensor, src.offset + (l * 4 * 512 + cch * 128) * 64, [[64, 128], [512 * 64, 4], [1, 64]])), w=[b_n2])
                S.op("dve", I("tensor_copy", out=vb16[k][:], in_=n2[:, 0:256]), r=[b_n2], w=[b_vb16[k]])
                S.dma("sp", I("dma_start", out=scr[dst][col0:col0 + 128, :], in_=vb16[k][:]), r=[b_vb16[k]], w=[bb[jj]])

    def lru_phase(l):
        P_ = LP[l]
        prm, b_prm, h0t, b_h0, c8, b_c8, wbd, b_wbd = P_["prm"], P_["b_prm"], P_["h0t"], P_["b_h0"], P_["c8"], P_["b_c8"], P_["wbd"], P_["b_wbd"]
        c8b = P_["c8b"]
        TM = NS_T
        lxb = sb([128, TM + 16], F32, "lxb"); b_lxb = Buf()
        xc = sb([128, TM], F32, "xc"); b_xc = Buf()
        xcb = sb([128, TM], BF16, "xcb"); b_xcb = Buf()
        R = [sb([128, TM], F32, "R") for _ in range(2)]; b_R = [Buf(), Buf()]
        Ib = [sb([128, TM], F32, "Ib") for _ in range(2)]; b_Ib = [Buf(), Buf()]
        T2 = [sb([128, TM], F32, "T2") for _ in range(2)]; b_T2 = [Buf(), Buf()]
        HH = T2; b_HH = b_T2
        G = sb([128, TM], BF16, "G"); b_G = Buf()
        Y = sb([128, TM], BF16, "Y"); b_Y = Buf()
        LGRP = ((0, 4, 256), (NP_T, 1, NS_T))
        lx_loaded = set()

        def lx_load(g_, c_):
            t0_, nseq_, T_ = LGRP[g_]
            N_ = nseq_ * T_
            TP_ = T_ + 4
            tiles_ = list(range(t0_ // 512, (t0_ + N_ + 511) // 512))
            v3 = lxb[:, 0:nseq_ * TP_].rearrange("p (s t) -> p s t", t=TP_)
            S.op("pool", I("memset", v3[:, :, 0:1], 0.0), w=[b_lxb])
            S.op("pool", I("memset", v3[:, :, T_ + 1:T_ + 4], 0.0), w=[b_lxb])
            S.dma("sp", I("dma_start", out=v3[:, :, 1:T_ + 1], in_=scr["LX"][c_ * 128:(c_ + 1) * 128, t0_:t0_ + N_].rearrange("p (s t) -> p s t", t=T_)),
                  r=[bLX[t] for t in tiles_], w=[b_lxb])
            lx_loaded.add((g_, c_))

        for gi_, (t0, nseq, T) in enumerate(LGRP):
            N = nseq * T
            TP = T + 4
            tiles = list(range(t0 // 512, (t0 + N + 511) // 512))
            for c in range(4):
                lx3 = lxb[:, 0:nseq * TP].rearrange("p (s t) -> p s t", t=TP)
                xc3 = xc[:, 0:N].rearrange("p (s t) -> p s t", t=T)
                if not (gi_, c) in lx_loaded:
                    lx_load(gi_, c)
                S.dma("sp", I("dma_start", out=G[:, 0:N], in_=scr["LG"][c * 128:(c + 1) * 128, t0:t0 + N]), r=[bLG[t] for t in tiles], w=[b_G])
                S.op("dve", I("tensor_scalar", out=xc3, in0=lx3[:, :, 0:T], scalar1=prm[:, c, 0:1], scalar2=prm[:, c, 4:5], op0=ALU.mult, op1=ALU.add),
                     r=[b_lxb, b_prm], w=[b_xc])
                for tap in range(1, 4):
                    S.op("dve", I("scalar_tensor_tensor", out=xc3, in0=lx3[:, :, tap:tap + T], scalar=prm[:, c, tap:tap + 1], in1=xc3, op0=ALU.mult, op1=ALU.add),
                         r=[b_lxb, b_prm, b_xc], w=[b_xc])
                S.op("act", I("activation", out=xcb[:, 0:N], in_=xc[:, 0:N], func=AF.Copy), r=[b_xc], w=[b_xcb])
                nxt = (gi_, c + 1) if c < 3 else ((gi_ + 1, 0) if gi_ == 0 else None)
                if nxt is not None:
                    lx_load(*nxt)

                def st_gates(dr, c=c, N=N):
                    for tl in range(N // 512):
                        cs = slice(tl * 512, (tl + 1) * 512)
                        for gate, (dst, bd, boff) in enumerate(((R[dr], b_R[dr], 5), (Ib[dr], b_Ib[dr], 7))):
                            gi = (c * 2 + dr) * 2 + gate
                            ps, pb = PS()
                            S.op("pe", I("matmul", ps[:], lhsT=wbd[:, gi, :], rhs=xcb[:, cs], start=True, stop=True), r=[b_wbd, b_xcb], w=[pb])
                            S.op("act", I("activation", out=dst[:, cs], in_=ps[:], func=AF.Sigmoid, bias=prm[:, c, boff + dr:boff + dr + 1]), r=[pb, b_prm], w=[bd])

                def st_exp(dr, c=c, N=N):
                    S.op("act", I("activation", out=R[dr][:, 0:N], in_=R[dr][:, 0:N], func=AF.Exp, scale=c8[:, c, dr:dr + 1]), r=[b_R[dr], b_c8], w=[b_R[dr]])

                def st_sq(dr, c=c, N=N):
                    S.op("act", I("activation", out=T2[dr][:, 0:N], in_=R[dr][:, 0:N], func=AF.Exp, scale=c8b[:, c, dr:dr + 1]), r=[b_R[dr], b_c8], w=[b_T2[dr]])

                def st_sqrt(dr, N=N):
                    S.op("act", I("activation", out=T2[dr][:, 0:N], in_=T2[dr][:, 0:N], func=AF.Sqrt, scale=-1.0, bias=1.0), r=[b_T2[dr]], w=[b_T2[dr]])

                def st_b1(dr, N=N):
                    S.op("dve", I("tensor_tensor", out=Ib[dr][:, 0:N], in0=Ib[dr][:, 0:N], in1=xc[:, 0:N], op=ALU.mult), r=[b_Ib[dr], b_xc], w=[b_Ib[dr]])

                def st_b2(dr, N=N):
                    S.op("dve", I("tensor_tensor", out=Ib[dr][:, 0:N], in0=Ib[dr][:, 0:N], in1=T2[dr][:, 0:N], op=ALU.mult), r=[b_Ib[dr], b_T2[dr]], w=[b_Ib[dr]])

                def st_scan(dr, c=c, nseq=nseq, T=T, gi_=gi_):
                    for sq_ in range(nseq):
                        o0 = sq_ * T
                        init = h0t[:, c, dr:dr + 1] if gi_ == 1 else 0.0
                        if dr == 0:
                            S.op("dve", I("tensor_tensor_scan", out=HH[0][:, o0:o0 + T], data0=R[0][:, o0:o0 + T], data1=Ib[0][:, o0:o0 + T], initial=init, op0=ALU.mult, op1=ALU.add),
                                 r=[b_R[0], b_Ib[0], b_h0], w=[b_HH[0]])
                        else:
                            def rv(t_):
                                return mkap(t_[:], o0 + T - 1, [[pst(t_), 128], [-1, T]])
                            S.op("dve", I("tensor_tensor_scan", out=rv(HH[1]), data0=rv(R[1]), data1=rv(Ib[1]), initial=init, op0=ALU.mult, op1=ALU.add),
                                 r=[b_R[1], b_Ib[1], b_h0], w=[b_HH[1]])
                        if gi_ == 0:
                            o = dout["o_lru"]
                            col = o0 + T - 1 if dr == 0 else o0
                            S.dma("sp", I("dma_start", out=bass.AP(o.tensor, o.offset + ((sq_ * 2 + l) * 2 + dr) * 512 + c * 128, [[1, 128], [1, 1]]), in_=HH[dr][:, col:col + 1]), r=[b_HH[dr]])

                for step in (st_gates, st_sq, st_exp, st_b1, st_sqrt, st_b2, st_scan):
                    for dr in range(2):
                        step(dr)
                S.op("dve", I("tensor_tensor", out=HH[0][:, 0:N], in0=HH[0][:, 0:N], in1=HH[1][:, 0:N], op=ALU.add), r=[b_HH[0], b_HH[1]], w=[b_HH[0]])
                S.op("dve", I("tensor_tensor", out=Y[:, 0:N], in0=HH[0][:, 0:N], in1=G[:, 0:N], op=ALU.mult), r=[b_HH[0], b_G], w=[b_Y])
                S.dma("sp", I("dma_start", out=scr["CATT"][c * 128:(c + 1) * 128, t0:t0 + N], in_=Y[:, 0:N]), r=[b_Y],
                      w=[bCATT[2 * c][t] for t in tiles] + [bCATT[2 * c + 1][t] for t in tiles])

    def attn_phase(l, lam_init):
        P_ = LP[l]
        nlam, b_nlam, gsub, b_gsub = P_["nlam"], P_["b_nlam"], P_["gsub"], P_["b_gsub"]
        bZR = bZRl[l]
        zr_l = scr["ZR"][l]
        osb = [sb([65, 512], F32, "osb") for _ in range(2)]; b_osb = [Buf(), Buf()]
        rd = [sb([64, 512], F32, "rd") for _ in range(2)]; b_rd = [Buf(), Buf()]
        on = [sb([64, 512], F32, "on") for _ in range(2)]; b_on = [Buf(), Buf()]
        yv = sb([64, 512], F32, "yv"); b_yv = Buf()
        ysq = sb([64, 512], F32, "ysq"); b_ysq = Buf()
        rr = sb([64, 512], F32, "rr"); b_rr = Buf()
        y16 = [sb([64, 512], BF16, "y16") for _ in range(2)]; b_y16 = [Buf(), Buf()]
        NEB = 6
        ebuf = [sb([128, 512], BF16, "ebuf") for _ in range(NEB)]; b_ebuf = [Buf() for _ in range(NEB)]
        ecnt = [0]
        ycnt = [0]

        def attn_block(QT_ap, nq, chunks, acc_bank, b_q, b_k, b_v, stages=None):
            po, pob = psum[acc_bank], psb[acc_bank]
            n = len(chunks)
            LA = 4 if any(ch[2] is not None for ch in chunks) else min(4, len(ps_banks[0]) - 1)
            slots = [None] * n
            stages = list(stages or [])

            def issue_qk(i):
                kt = chunks[i][0]
                ps, pb = PS()
                S.op("pe", I("matmul", ps[:, 0:nq], lhsT=kt, rhs=QT_ap, start=True, stop=True), r=[b_q, b_k], w=[pb])
                e = ecnt[0] % NEB
                ecnt[0] += 1
                S.op("act", I("activation", out=ebuf[e][:, 0:nq], in_=ps[:, 0:nq], func=AF.Exp), r=[pb], w=[b_ebuf[e]])
                mk, mkb = chunks[i][2], chunks[i][3]
                if mk is not None:
                    S.op("dve", I("tensor_tensor", out=ebuf[e][:, 0:nq], in0=ebuf[e][:, 0:nq], in1=mk, op=ALU.mult), r=[b_ebuf[e], mkb], w=[b_ebuf[e]])
                slots[i] = e

            for i in range(min(LA, n)):
                issue_qk(i)
            for i in range(n):
                if i + LA < n:
                    issue_qk(i + LA)
                e = slots[i]
                S.op("pe", I("matmul", po[0:65, 0:nq], lhsT=chunks[i][1], rhs=ebuf[e][:, 0:nq], start=(i == 0), stop=(i == n - 1)), r=[b_v, b_ebuf[e]], w=[pob])
                while stages and stages[0][0] * n <= i + 1:
                    stages.pop(0)[1]()
            while stages:
                stages.pop(0)[1]()

        def norm_stages(acc_bank, nq, k, recip_on_act=False):
            po, pob = psum[acc_bank], psb[acc_bank]

            def s0():
                S.op("act", I("activation", out=osb[k][:, 0:nq], in_=po[0:65, 0:nq], func=AF.Copy), r=[pob], w=[b_osb[k]])

            def s1():
                if recip_on_act:
                    S.op("act", I("activation", out=osb[k][64:65, 0:nq], in_=osb[k][64:65, 0:nq], func=AF.Ln), r=[b_osb[k]], w=[b_osb[k]])
                    S.op("act", I("activation", out=osb[k][64:65, 0:nq], in_=osb[k][64:65, 0:nq], func=AF.Exp, scale=-1.0), r=[b_osb[k]], w=[b_osb[k]])
                else:
                    S.op("dve", I("reciprocal", out=osb[k][64:65, 0:nq], in_=osb[k][64:65, 0:nq]), r=[b_osb[k]], w=[b_osb[k]])

            def s2():
                ps, pb = PS()
                S.op("pe", I("matmul", ps[0:64, 0:nq], lhsT=sel65[:], rhs=osb[k][:, 0:nq], start=True, stop=True), r=[b_sel65, b_osb[k]], w=[pb])
                S.op("dve", I("tensor_tensor", out=on[k][:, 0:nq], in0=osb[k][0:64, 0:nq], in1=ps[0:64, 0:nq], op=ALU.mult), r=[b_osb[k], pb], w=[b_on[k]])
            return s0, s1, s2

        def normalize(acc_bank, nq, k):
            po, pob = psum[acc_bank], psb[acc_bank]
            S.op("act", I("activation", out=osb[k][:, 0:nq], in_=po[0:65, 0:nq], func=AF.Copy), r=[pob], w=[b_osb[k]])
            S.op("dve", I("reciprocal", out=osb[k][64:65, 0:nq], in_=osb[k][64:65, 0:nq]), r=[b_osb[k]], w=[b_osb[k]])
            ps, pb = PS()
            S.op("pe", I("matmul", ps[0:64, 0:nq], lhsT=sel65[:], rhs=osb[k][:, 0:nq], start=True, stop=True), r=[b_sel65, b_osb[k]], w=[pb])
            S.op("dve", I("tensor_tensor", out=on[k][:, 0:nq], in0=osb[k][0:64, 0:nq], in1=ps[0:64, 0:nq], op=ALU.mult), r=[b_osb[k], pb], w=[b_on[k]])

        def store_cat(src16_ap, src_b, row0, col0, nq):
            t = col0 // 512
            S.dma("sp", I("dma_start", out=scr["CATT"][row0:row0 + 64, col0:col0 + nq], in_=src16_ap), r=[src_b], w=[bCATT[row0 // 64][t]])

        def na_chunks_for(Q):
            lo, hi = max(0, 4 * Q - 2), min(31, 4 * Q + 5)
            return list(range(lo, hi + 1))

        def tile_class(Q):
            return 0 if Q == 0 else (2 if Q == 7 else 1)

        def valid_b_range(kr, Q):
            bs = [b for b in range(8) if min(max(8 * Q + b - 4, 0), 56) <= kr < min(max(8 * Q + b - 4, 0), 56) + 8]
            return (bs[0], bs[-1]) if bs else None

        def mask_tile_idx(Q, m):
            c = tile_class(Q)
            if c == 0:
                return m
            if c == 1:
                return 6 + (m - (4 * Q - 2))
            return 14 + (m - 26)

        with phase():
            nsets = []
            for _i in range(2):
                d_ = dict(QT=sb([128, TT], BF16, "QT"), KT=sb([128, TT + L_CTX], BF16, "KT"), VA=sb([128, 44, 65], BF16, "VA"),
                          MT=sb([128, 20, 512], BF16, "MT"), b_QT=Buf(), b_KT=Buf(), b_VA=Buf(), b_MT=MB(), b_MTz=Buf())
                S.op("pool", I("memset", d_["QT"][:], 0.0), w=[d_["b_QT"]])
                S.op("pool", I("memset", d_["KT"][:], 0.0), w=[d_["b_KT"]])
                S.op("pool", I("memset", d_["VA"][:, :, 64:65], 1.0), w=[d_["b_VA"]])
                nsets.append(d_)

            def na_load(h):
                d_ = nsets[h % 2]
                QT, KT, VA, MT = d_["QT"], d_["KT"], d_["VA"], d_["MT"]
                S.dma("sp", I("dma_start", out=QT[0:64, :], in_=scr["NQT"][h * 64:(h + 1) * 64, :]), r=bNQT, w=[d_["b_QT"]])
                S.dma("sp", I("dma_start", out=KT[0:64, :], in_=scr["NKT"][h * 64:(h + 1) * 64, :]), r=bNKT, w=[d_["b_KT"]])
                S.dma("sp", I("dma_start", out=VA[:, :, 0:64], in_=scr["NV"][:, h * 64:(h + 1) * 64].rearrange("(c p) d -> p c d", p=128)), r=bNV, w=[d_["b_VA"]])
                S.op("pool", I("memset", MT[:], 0.0), w=[d_["b_MT"], d_["b_MTz"]])
                zr = zr_l
                for Q in (0, 1, 7):
                    for m in na_chunks_for(Q):
                        ti = mask_tile_idx(Q, m)
                        for a_ in range(2):
                            kr = 2 * m + a_
                            vr = valid_b_range(kr, Q)
                            if vr is None:
                                continue
                            b0, b1 = vr
                            nbv = b1 - b0 + 1
                            j0 = 7 - kr + 8 * Q + b0
                            srcap = bass.AP(zr.tensor, zr.offset + (h * 15 + j0) * 4096, [[64, 64], [4096, nbv], [1, 64]])
                            dstap = MT[a_ * 64:(a_ + 1) * 64, ti, b0 * 64:(b1 + 1) * 64].rearrange("p (b q) -> p b q", q=64)
                            S.dma("sp", I("dma_start", out=dstap, in_=srcap), r=[bZR, d_["b_MTz"]], w=[d_["b_MT"]])

            na_pend = []
            nacnt = [0]
            ps_banks[0] = list(range(6))
            na_load(0)
            for h in range(4):
                if h + 1 < 4:
                    na_load(h + 1)
                d_ = nsets[h % 2]
                QT, KT, VA, MT = d_["QT"], d_["KT"], d_["VA"], d_["MT"]
                b_QT, b_KT, b_VA, b_MT = d_["b_QT"], d_["b_KT"], d_["b_VA"], d_["b_MT"]
                blocks = []
                for s_ in range(4):
                    chunks = [(KT[:, (2 * s_ + i) * 128:(2 * s_ + i + 1) * 128], VA[:, 2 * s_ + i, :], None, None) for i in range(2)]
                    blocks.append((QT[:, s_ * 256:(s_ + 1) * 256], 256, chunks, s_ * 256))
                for Q in range(8):
                    chunks = []
                    for m in na_chunks_for(Q):
                        ti = mask_tile_idx(Q, m)
                        chunks.append((KT[:, NP_T + m * 128:NP_T + (m + 1) * 128], VA[:, 8 + m, :], MT[:, ti, :], b_MT))
                    for i in range(4):
                        chunks.append((KT[:, TT + i * 128:TT + (i + 1) * 128], VA[:, 40 + i, :], None, None))
                    blocks.append((QT[:, NP_T + Q * 512:NP_T + (Q + 1) * 512], 512, chunks, NP_T + Q * 512))
                for (qap, nq, chunks, q0) in blocks:
                    bank = 6 + (nacnt[0] % 2)
                    nacnt[0] += 1
                    attn_block(qap, nq, chunks, bank, b_QT, b_KT, b_VA, stages=(na_pend.pop(0) if na_pend else None))

                    def mk(bank=bank, nq=nq, q0=q0, h=h):
                        s0, s1, s2 = norm_stages(bank, nq, 0, recip_on_act=True)

                        def s3():
                            k = ycnt[0] % 2; ycnt[0] += 1
                            S.op("pool", I("tensor_copy", out=y16[k][:, 0:nq], in_=on[0][:, 0:nq]), r=[b_on[0]], w=[b_y16[k]])
                            store_cat(y16[k][:, 0:nq], b_y16[k], 512 + h * 64, q0, nq)
                        return [(0.1, s0), (0.15, s1), (0.8, s2), (0.95, s3)]
                    na_pend.append(mk())
            while na_pend:
                for (_f, fn) in na_pend.pop(0):
                    fn()
        if stage <= 3:
            return
        with phase():
            dsets = []
            for _i in range(2):
                d_ = dict(DQ=sb([128, 2, TT], BF16, "DQ"), DK=sb([128, 2, TT + L_CTX], BF16, "DK"), VA=sb([128, 44, 65], BF16, "VAd"),
                          b_DQ=Buf(), b_DK=Buf(), b_VA=Buf())
                S.op("pool", I("memset", d_["DQ"][:], 0.0), w=[d_["b_DQ"]])
                S.op("pool", I("memset", d_["DK"][:], 0.0), w=[d_["b_DK"]])
                S.op("pool", I("memset", d_["VA"][:, :, 64:65], 1.0), w=[d_["b_VA"]])
                dsets.append(d_)

            def df_load(h):
                d_ = dsets[h % 2]
                S.dma("sp", I("dma_start", out=d_["DQ"][0:32, :, :], in_=scr["DQT"][h * 64:(h + 1) * 64, :].rearrange("(c d) t -> d c t", d=32)), r=bDQT, w=[d_["b_DQ"]])
                S.dma("sp", I("dma_start", out=d_["DK"][0:32, :, :], in_=scr["DKT"][h * 64:(h + 1) * 64, :].rearrange("(c d) t -> d c t", d=32)), r=bDKT, w=[d_["b_DK"]])
                S.dma("sp", I("dma_start", out=d_["VA"][:, :, 0:64], in_=scr["DV"][:, h * 64:(h + 1) * 64].rearrange("(c p) d -> p c d", p=128)), r=bDV, w=[d_["b_VA"]])

            df_pend = []
            dfcnt = [0]
            dfacc = [0]
            ps_banks[0] = list(range(5))
            df_load(0)
            for h in range(4):
                if h + 1 < 4:
                    df_load(h + 1)
                d_ = dsets[h % 2]
                DQ, DK, VA = d_["DQ"], d_["DK"], d_["VA"]
                b_DQ, b_DK, b_VA = d_["b_DQ"], d_["b_DK"], d_["b_VA"]
                blocks = [(s * 256, 256, [2 * s, 2 * s + 1]) for s in range(4)] + \
                         [(NP_T + Q * 512, 512, list(range(8, 44))) for Q in range(8)]
                for (q0, nq, chs) in blocks:
                    par = dfcnt[0] % 2
                    dfcnt[0] += 1
                    pend = df_pend.pop(0) if df_pend else [None, None]
                    banks = []
                    for c in range(2):
                        chunks = [(DK[:, c, ch * 128:(ch + 1) * 128], VA[:, ch, :], None, None) for ch in chs]
                        bank = 5 + (dfacc[0] % 3)
                        dfacc[0] += 1
                        banks.append(bank)
                        attn_block(DQ[:, c, q0:q0 + nq], nq, chunks, bank, b_DQ, b_DK, b_VA, stages=pend[c])

                    def mk(banks=banks, nq=nq, q0=q0, h=h):
                        a0, a1, a2 = norm_stages(banks[0], nq, 0)
                        b0, b1, b2 = norm_stages(banks[1], nq, 1)

                        def t0():
                            S.op("dve", I("scalar_tensor_tensor", out=yv[:, 0:nq], in0=on[1][:, 0:nq], scalar=nlam[0:64, 0:1], in1=on[0][:, 0:nq], op0=ALU.mult, op1=ALU.add),
                                 r=[b_on[0], b_on[1], b_nlam], w=[b_yv])
                            S.op("pool", I("tensor_tensor", out=ysq[:, 0:nq], in0=yv[:, 0:nq], in1=yv[:, 0:nq], op=ALU.mult), r=[b_yv], w=[b_ysq])

                        def t1():
                            ps, pb = PS()
                            S.op("pe", I("matmul", ps[0:64, 0:nq], lhsT=ones64[:], rhs=ysq[:, 0:nq], start=True, stop=True), r=[b_ones64, b_ysq], w=[pb])
                            S.op("act", I("activation", out=rr[:, 0:nq], in_=ps[0:64, 0:nq], func=AF.Ln, scale=1.0 / 64, bias=eps_t[0:64, 0:1]), r=[pb, b_eps], w=[b_rr])
                            S.op("act", I("activation", out=rr[:, 0:nq], in_=rr[:, 0:nq], func=AF.Exp, scale=-0.5), r=[b_rr], w=[b_rr])

                        def t2():
                            k = ycnt[0] % 2; ycnt[0] += 1
                            S.op("dve", I("scalar_tensor_tensor", out=y16[k][:, 0:nq], in0=yv[:, 0:nq], scalar=gsub[:, 0:1], in1=rr[:, 0:nq], op0=ALU.mult, op1=ALU.mult),
                                 r=[b_yv, b_gsub, b_rr], w=[b_y16[k]])
                            store_cat(y16[k][:, 0:nq], b_y16[k], 768 + h * 64, q0, nq)
                        first = [(0.03, a0), (0.05, b0), (0.08, a1), (0.1, b1), (0.5, a2), (0.6, b2), (0.75, t0)]
                        second = [(0.1, t1), (0.5, t2)]
                        return [first, second]
                    df_pend.append(mk())
            while df_pend:
                for lst in df_pend.pop(0):
                    for (_f, fn) in lst:
                        fn()

    def ffn_phase(l, x_src, bx_src, y_dst, by_dst):
        wstack = contextlib.ExitStack()
        hstack = contextlib.ExitStack()
        prev_stack = cur[0]
        cur[0] = hstack
        hT = sb([128, 8, TT], BF16, "h2T")
        cur[0] = prev_stack
        b_hT = [Buf() for _ in range(NSUB)]
        with phase():
            for g in range(2):
                load_mod(l, g, 2, g)
                load_gs(l, g, 4, "norm_ffn", 2 + g)
                load_mod(l, g, 3, 4 + g)
            wout = sb([128, 8, D], BF16, "wout"); b_wout = MB()
            for hf in range(2):
                S.dma("pool", I("dma_start", out=wout[:, :, hf * 512:(hf + 1) * 512], in_=din["w_out"][l][:, hf * 512:(hf + 1) * 512].rearrange("(k p) n -> p k n", p=128)), w=[b_wout])
            wks = [mk_wk(), mk_wk(), mk_wk()]
            xts = [sb([128, D], F32, "xt5") for _ in range(3)]; bxts = [Buf(), Buf(), Buf()]
            xms = [sb([128, D], F32, "xm") for _ in range(3)]; bxms = [Buf(), Buf(), Buf()]
            cts = [sb([128, 8, 128], BF16, "ct") for _ in range(3)]; bcts = [Buf(), Buf(), Buf()]
            tmps = [sb([128, 512], F32, "tmp5") for _ in range(2)]; b_tmps = [Buf(), Buf()]

            def p5a_load(j):
                k = j % 3
                S.dma("sp", I("dma_start", out=xts[k][:], in_=x_src[j * 128:(j + 1) * 128, :]), r=[bx_src[j]], w=[bxts[k]])
                S.dma("sp", I("dma_start", out=cts[k][:], in_=scr["CATT"][:, j * 128:(j + 1) * 128].rearrange("(k p) t -> p k t", p=128)),
                      r=[bCATT[c][j // 4] for c in range(16)], w=[bcts[k]])

            p5a_load(0)
            p5a_load(1)

            def p5a_A(j):
                g = 0 if j < 8 else 1
                k = j % 3
                if j + 2 < NSUB:
                    p5a_load(j + 2)
                for hf in range(2):
                    ps, pb = PS()
                    for kc in range(8):
                        S.op("pe", I("matmul", ps[:], lhsT=cts[k][:, kc, :], rhs=wout[:, kc, hf * 512:(hf + 1) * 512], start=(kc == 0), stop=(kc == 7)),
                             r=[bcts[k], b_wout], w=[pb])
                    S.op("dve", I("tensor_tensor", out=tmps[hf][:], in0=ps[:], in1=mpool[g][:, hf * 512:(hf + 1) * 512], op=ALU.mult), r=[pb, mpool_b[g]], w=[b_tmps[hf]])
                    S.op("pool", I("tensor_tensor", out=xms[j % 3][:, hf * 512:(hf + 1) * 512], in0=tmps[hf][:], in1=xts[k][:, hf * 512:(hf + 1) * 512], op=ALU.add),
                         r=[b_tmps[hf], bxts[k]], w=[bxms[j % 3]])
                S.dma("pool", I("dma_start", out=scr["XMID"][j * 128:(j + 1) * 128, :], in_=xms[j % 3][:]), r=[bxms[j % 3]], w=[bXMID[j]])

            def p5a_A2(j):
                g = 0 if j < 8 else 1
                k = j % 3
                rms_mod(xms[k], bxms[k], mpool[2 + g], mpool[4 + g], mpool_b[2 + g], mpool_b[4 + g], wks[j % 3])

            def p5a_B(j):
                transpose_to_hT(j, wks[j % 3], hT, b_hT)

            pipeline(NSUB, [p5a_A, p5a_A2, p5a_B], skew=1)
        if stage <= 4:
            hstack.close()
            return
        cur[0] = wstack
        wd = sb([128, 22, D], BF16, "wd"); b_wd = MB()
        cur[0] = prev_stack
        with phase():
            for q in range(2):
                S.dma("pool", I("dma_start", out=wd[:, q * 11:(q + 1) * 11, :], in_=din["ffn_down"][l][q * 11 * 128:(q + 1) * 11 * 128, :].rearrange("(c p) n -> p c n", p=128)), w=[b_wd])
            wp = [sb([128, 8, 256], BF16, "wp") for _ in range(2)]; b_wp = [MB(), MB()]
            cw = [sb([128, 2, 4], F32, "cw") for _ in range(2)]; b_cw = [MB(), MB()]
            NCV = 3
            cv = [[sb([128, 256], F32, "cv") for _ in range(2)] for _ in range(NCV)]; b_cv = [[Buf(), Buf()] for _ in range(NCV)]
            gg = [sb([128, 256], F32, "gg") for _ in range(2)]; b_gg = [Buf(), Buf()]
            a16 = [sb([128, 256], BF16, "a16") for _ in range(NCV)]; b_a16 = [Buf() for _ in range(NCV)]
            fcnt = 0

            def p5b_load(v):
                k = v % 2
                for part in range(2):
                    c0 = (part * 22 + v) * 128
                    S.dma("pool", I("dma_start", out=wp[k][:, :, part * 128:(part + 1) * 128], in_=din["ffn_up"][l][:, c0:c0 + 128].rearrange("(k p) n -> p k n", p=128)), w=[b_wp[k]])
                    src = din["ffn_conv_w"]
                    S.dma("sp", I("dma_start", out=cw[k][:, part, 0:3], in_=bass.AP(src.tensor, src.offset + l * 3 * 2 * D_FF + c0, [[1, 128], [2 * D_FF, 3]])), w=[b_cw[k]])
                    src = din["ffn_conv_b"]
                    S.dma("sp", I("dma_start", out=cw[k][:, part, 3:4], in_=bass.AP(src.tensor, src.offset + l * 2 * D_FF + c0, [[1, 128], [1, 1]])), w=[b_cw[k]])

            pend = []

            def p5b_B(f, fg, v, tok0):
                S.op("act", I("activation", out=gg[fg][:], in_=cv[f][1][:], func=AF.Gelu_apprx_tanh), r=[b_cv[f][1]], w=[b_gg[fg]])
                S.op("pool", I("tensor_tensor", out=a16[f][:], in0=gg[fg][:], in1=cv[f][0][:], op=ALU.mult), r=[b_gg[fg], b_cv[f][0]], w=[b_a16[f]])
                S.dma("sp", I("dma_start", out=scr["ACTT"][v * 128:(v + 1) * 128, tok0:tok0 + 256], in_=a16[f][:]), r=[b_a16[f]], w=[bACTT[v][tok0 // 512]])

            p5b_load(0)
            for v in range(22):
                k = v % 2
                if v + 1 < 22:
                    p5b_load(v + 1)
                for (t0, T) in SEQS:
                    nft = T // 256
                    for ft in range(nft):
                        tok0 = t0 + ft * 256
                        lo = 1 if ft > 0 else 0
                        hi = 1 if ft < nft - 1 else 0
                        c0 = tok0 - lo
                        ncol = 256 + lo + hi
                        f = fcnt % NCV
                        fg = fcnt % 2
                        fcnt += 1
                        subs = sorted(set([c0 // 128, (c0 + ncol - 1) // 128, tok0 // 128, tok0 // 128 + 1]))
                        for part in range(2):
                            ps, pb = PS()
                            for kc in range(8):
                                S.op("pe", I("matmul", ps[:, 0:ncol], lhsT=wp[k][:, kc, part * 128:(part + 1) * 128], rhs=hT[:, kc, c0:c0 + ncol], start=(kc == 0), stop=(kc == 7)),
                                     r=[b_wp[k]] + [b_hT[s_] for s_ in subs], w=[pb])
                            c_ = cv[f][part]; bc = b_cv[f][part]
                            S.op("act", I("activation", out=c_[:], in_=ps[:, lo:lo + 256], func=AF.Identity, scale=cw[k][:, part, 1:2], bias=cw[k][:, part, 3:4]),
                                 r=[pb, b_cw[k]], w=[bc])
                            if lo:
                                S.op("dve", I("scalar_tensor_tensor", out=c_[:], in0=ps[:, 0:256], scalar=cw[k][:, part, 0:1], in1=c_[:], op0=ALU.mult, op1=ALU.add),
                                     r=[pb, b_cw[k], bc], w=[bc])
                            else:
                                S.op("dve", I("scalar_tensor_tensor", out=c_[:, 1:256], in0=ps[:, 0:255], scalar=cw[k][:, part, 0:1], in1=c_[:, 1:256], op0=ALU.mult, op1=ALU.add),
                                     r=[pb, b_cw[k], bc], w=[bc])
                            if hi:
                                S.op("dve", I("scalar_tensor_tensor", out=c_[:], in0=ps[:, lo + 1:lo + 257], scalar=cw[k][:, part, 2:3], in1=c_[:], op0=ALU.mult, op1=ALU.add),
                                     r=[pb, b_cw[k], bc], w=[bc])
                            else:
                                S.op("dve", I("scalar_tensor_tensor", out=c_[:, 0:255], in0=ps[:, lo + 1:lo + 256], scalar=cw[k][:, part, 2:3], in1=c_[:, 0:255], op0=ALU.mult, op1=ALU.add),
                                     r=[pb, b_cw[k], bc], w=[bc])
                        pend.append((f, fg, v, tok0))
                        if len(pend) > 1:
                            p5b_B(*pend.pop(0))
            while pend:
                p5b_B(*pend.pop(0))
        if stage <= 5:
            wstack.close()
            hstack.close()
            return
        with phase():
            for g in range(2):
                load_mod(l, g, 5, g)
            at = [sb([128, 22, 256], BF16, "at") for _ in range(2)]; b_at = [Buf(), Buf()]
            xms = [sb([128, D], F32, "xm6") for _ in range(3)]; bxms = [Buf() for _ in range(3)]
            xo = [sb([128, D], F32, "xo") for _ in range(2)]; bxo = [Buf(), Buf()]
            tmps = [sb([128, 512], F32, "tmp6") for _ in range(2)]; b_tmps = [Buf(), Buf()]
            NT2 = TT // 256

            def p5c_load_at(t):
                S.dma("sp", I("dma_start", out=at[t % 2][:], in_=scr["ACTT"][:, t * 256:(t + 1) * 256].rearrange("(c p) t -> p c t", p=128)),
                      r=[bACTT[v][t // 2] for v in range(22)], w=[b_at[t % 2]])

            def p5c_load_x(j):
                S.dma("sp", I("dma_start", out=xms[j % 3][:], in_=scr["XMID"][j * 128:(j + 1) * 128, :]), r=[bXMID[j]], w=[bxms[j % 3]])

            p5c_load_at(0)
            p5c_load_x(0)
            p5c_load_x(1)
            for t in range(NT2):
                ka = t % 2
                if t + 1 < NT2:
                    p5c_load_at(t + 1)
                for s_ in range(2):
                    j = t * 2 + s_
                    g = 0 if j < 8 else 1
                    k = j % 2
                    if j + 2 < NSUB:
                        p5c_load_x(j + 2)
                    for hf in range(2):
                        ps, pb = PS()
                        for c in range(22):
                            S.op("pe", I("matmul", ps[:], lhsT=at[ka][:, c, s_ * 128:(s_ + 1) * 128], rhs=wd[:, c, hf * 512:(hf + 1) * 512], start=(c == 0), stop=(c == 21)),
                                 r=[b_at[ka], b_wd], w=[pb])
                        S.op("dve", I("tensor_tensor", out=tmps[hf][:], in0=ps[:], in1=mpool[g][:, hf * 512:(hf + 1) * 512], op=ALU.mult), r=[pb, mpool_b[g]], w=[b_tmps[hf]])
                        S.op("pool", I("tensor_tensor", out=xo[k][:, hf * 512:(hf + 1) * 512], in0=tmps[hf][:], in1=xms[j % 3][:, hf * 512:(hf + 1) * 512], op=ALU.add),
                             r=[b_tmps[hf], bxms[j % 3]], w=[bxo[k]])
                    S.dma("act", I("dma_start", out=y_dst[j * 128:(j + 1) * 128, :], in_=xo[k][:]), r=[bxo[k]], w=[by_dst[j]])
        wstack.close()
        hstack.close()

    for l in range(2):
        x_src = din["xin"] if l == 0 else scr["XRES"]
        bx_src = [Buf() for _ in range(NSUB)] if l == 0 else bXRES
        y_dst = scr["XRES"] if l == 0 else dout["y"]
        by_dst = bXRES if l == 0 else [Buf() for _ in range(NSUB)]
        lam_init = 0.8 - 0.6 * float(np.exp(-0.3 * l))
        with phase():
            p1_phase(l, x_src, bx_src)
        if stage <= 1:
            return
        with phase():
            lru_phase(l)
        if stage <= 2:
            return
        with phase():
            attn_phase(l, lam_init)
        if stage <= 3:
            return
        with phase():
            ffn_phase(l, x_src, bx_src, y_dst, by_dst)
        if stage <= 6:
            return


def _constants():
    ident = np.eye(128, dtype=np.float32)
    p60 = np.zeros((60, 60), np.float32)
    for h in range(4):
        for j in range(15):
            p60[h * 15 + (14 - j), h * 15 + j] = 1.0
    oh = np.zeros((32, 64, 64), np.float32)
    for qc in range(64):
        cstart = min(max(qc - 8, 0), 48)
        for kc in range(64):
            if cstart <= kc < cstart + 16:
                oh[kc - qc + 15, kc, qc] = 1.0
            else:
                oh[31, kc, qc] = -30000.0
    t = np.arange(NS_T)
    freqs = (10000.0 ** (-np.arange(8, dtype=np.float32) / 8)).astype(np.float32)
    ar = (t // 64).astype(np.float32)[:, None] * freqs[None, :]
    ac = (t % 64).astype(np.float32)[:, None] * freqs[None, :]
    cr, sr, cc, sc = np.cos(ar), np.sin(ar), np.cos(ac), np.sin(ac)
    ropec = np.concatenate([cr, cr, cc, cc], axis=1).astype(np.float32)
    ropes = np.concatenate([-sr, sr, -sc, sc], axis=1).astype(np.float32)
    return dict(c_ident=ident, c_p60=p60, c_oh=oh.reshape(32, 4096), c_ropec=ropec, c_ropes=ropes)


_WEIGHT_NAMES = ["ada_w", "ada_b", "norm_mix", "norm_ffn", "w_in", "lru_conv_w", "lru_conv_b", "lru_wa", "lru_ba", "lru_wx",
                 "lru_bx", "lru_lambda", "na_q_norm", "na_k_norm", "na_rpb", "df_q_norm", "df_k_norm", "df_lambda", "df_subln",
                 "w_out", "ffn_up", "ffn_conv_w", "ffn_conv_b", "ffn_down"]


def make_in_map(inputs, core, consts):
    f = lambda a: np.ascontiguousarray(np.asarray(a, dtype=np.float32))
    b = core // 2
    m = {}
    m["xin"] = f(np.concatenate([np.asarray(inputs["x_prompt"])[4 * core:4 * core + 4].reshape(NP_T, D), np.asarray(inputs["x_sample"])[b]], axis=0))
    m["cvec"] = f(np.stack([np.asarray(inputs["c_ctx"]), np.asarray(inputs["c"])[b]], axis=0))
    m["state"] = f(np.asarray(inputs["state_lru"])[b])
    m["cnk"] = f(np.asarray(inputs["cache_na_k"])[b])
    m["cnv"] = f(np.asarray(inputs["cache_na_v"])[b])
    m["cdk"] = f(np.asarray(inputs["cache_df_k"])[b])
    m["cdv"] = f(np.asarray(inputs["cache_df_v"])[b])
    for n in _WEIGHT_NAMES:
        m[n] = f(inputs[n])
    m.update(consts)
    return m


_NC_CACHE = {}


def kernel(**inputs):
    n = 8
    if "nc" not in _NC_CACHE:
        _NC_CACHE["nc"] = build()
    nc = _NC_CACHE["nc"]
    consts = _constants()
    in_maps = [make_in_map(inputs, c, consts) for c in range(n)]
    res = run_bass_kernel_spmd(nc, in_maps, core_ids=list(range(n))).results
    y_p = np.concatenate([r["y"][:NP_T].reshape(4, 256, D) for r in res], axis=0)
    y_s = np.stack([res[2 * b]["y"][NP_T:] for b in range(4)], axis=0)
    cat = lambda k: np.concatenate([r[k] for r in res], axis=0)
    return (y_p, y_s, cat("o_lru"), cat("o_nk"), cat("o_nv"), cat("o_dk"), cat("o_dv"))
```
